# Optimizing a Trainium2 kernel written in Bass

```python
import functools
import jax, jax.numpy as jnp
from jax import lax
import numpy as np

D_MODEL = 2048
BATCH = 8
SEQ = 2048
DEPTH = 2

MIX_WIDTH = D_MODEL
HEAD_DIM = 128
Q_BLOCK = 128
SB_WIDTH = MIX_WIDTH // 2
SB_HEADS = SB_WIDTH // HEAD_DIM
SC_WIDTH = MIX_WIDTH - SB_WIDTH
SC_GROUPS = SC_WIDTH // HEAD_DIM
CONV_WIDTH = 3
CHUNK = 128
SG_WIDTH = MIX_WIDTH // 2
SG_GROUP_DIM = 128
SG_GROUPS = SG_WIDTH // SG_GROUP_DIM
FOX_WIDTH = MIX_WIDTH - SG_WIDTH
FOX_HEADS = FOX_WIDTH // HEAD_DIM
IN_AB = 3 * SB_WIDTH + 3 * SC_WIDTH
IN_CD = 2 * SG_WIDTH + 3 * FOX_WIDTH + FOX_HEADS
D_FF = 5632
EPS = 1e-6

kernel_name = "hybrid_stickbreak_shortconv_chunkgmlp_fox_block"


def rmsnorm(x, g):
    xf = x.astype(jnp.float32)
    y = xf * lax.rsqrt(jnp.mean(xf * xf, axis=-1, keepdims=True) + EPS)
    return (y * g.astype(jnp.float32)).astype(x.dtype)


def layernorm(x, g):
    xf = x.astype(jnp.float32)
    mu = jnp.mean(xf, axis=-1, keepdims=True)
    xc = xf - mu
    y = xc * lax.rsqrt(jnp.mean(xc * xc, axis=-1, keepdims=True) + EPS)
    return (y * g.astype(jnp.float32)).astype(x.dtype)


def causal_dwconv(x, w):
    K = w.shape[0]
    S = x.shape[1]
    xp = jnp.pad(x, ((0, 0), (K - 1, 0), (0, 0)))
    y = xp[:, 0:S] * w[0]
    for j in range(1, K):
        y = y + xp[:, j:j + S] * w[j]
    return y


def split_heads(t, n_heads):
    return t.reshape(t.shape[0], t.shape[1], n_heads, -1)


def stick_breaking_attention(q, k, v):
    S = q.shape[1]
    scale = HEAD_DIM ** -0.5
    outs = []
    for i in range(S // Q_BLOCK):
        q0 = i * Q_BLOCK
        kend = q0 + Q_BLOCK
        qb = q[:, q0:kend].astype(jnp.float32)
        kb = k[:, :kend].astype(jnp.float32)
        vb = v[:, :kend].astype(jnp.float32)
        z = jnp.einsum('bqhd,bkhd->bhqk', qb, kb) * scale
        t_idx = q0 + jnp.arange(Q_BLOCK)[:, None]
        s_idx = jnp.arange(kend)[None, :]
        mask = s_idx < t_idx
        log_1mb = jnp.where(mask, jax.nn.log_sigmoid(-z), 0.0)
        later = lax.cumsum(log_1mb, axis=3, reverse=True) - log_1mb
        a = jnp.where(mask, jnp.exp(jax.nn.log_sigmoid(z) + later), 0.0)
        outs.append(jnp.einsum('bhqk,bkhd->bqhd', a, vb))
    return jnp.concatenate(outs, axis=1).astype(q.dtype)


def forgetting_attention(q, k, v, log_f):
    S = q.shape[1]
    scale = HEAD_DIM ** -0.5
    c = jnp.cumsum(log_f, axis=1).transpose(0, 2, 1)
    outs = []
    for i in range(S // Q_BLOCK):
        q0 = i * Q_BLOCK
        kend = q0 + Q_BLOCK
        qb = q[:, q0:kend].astype(jnp.float32)
        kb = k[:, :kend].astype(jnp.float32)
        vb = v[:, :kend].astype(jnp.float32)
        logits = jnp.einsum('bqhd,bkhd->bhqk', qb, kb) * scale
        logits = logits + c[:, :, q0:kend, None] - c[:, :, None, :kend]
        t_idx = q0 + jnp.arange(Q_BLOCK)[:, None]
        s_idx = jnp.arange(kend)[None, :]
        p = jax.nn.softmax(jnp.where(s_idx <= t_idx, logits, -jnp.inf), axis=-1)
        outs.append(jnp.einsum('bhqk,bkhd->bqhd', p, vb))
    return jnp.concatenate(outs, axis=1).astype(q.dtype)


def chunked_spatial_gate(u, v, w_s, b_s, g):
    B, S, W = v.shape
    v = layernorm(v, g)
    vc = v.reshape(B, S // CHUNK, CHUNK, SG_GROUPS, SG_GROUP_DIM)
    w = w_s * jnp.tril(jnp.ones((CHUNK, CHUNK), w_s.dtype))
    mixed = jnp.einsum('gts,bnsgc->bntgc', w, vc) + b_s.T[None, None, :, :, None]
    return u * mixed.reshape(B, S, W)


def mixer_ab(h, w_in, sc_conv_w, w_out):
    B, S, _ = h.shape
    p = h @ w_in
    q, k, v, gate_b, gate_c, hin = jnp.split(
        p, [SB_WIDTH, 2 * SB_WIDTH, 3 * SB_WIDTH,
            3 * SB_WIDTH + SC_WIDTH, 3 * SB_WIDTH + 2 * SC_WIDTH], axis=-1)
    a_out = stick_breaking_attention(split_heads(q, SB_HEADS), split_heads(k, SB_HEADS),
                                     split_heads(v, SB_HEADS)).reshape(B, S, SB_WIDTH)
    b_out = gate_b * causal_dwconv(gate_c * hin, sc_conv_w)
    return jnp.concatenate([a_out, b_out], axis=-1) @ w_out


def mixer_cd(h, w_in, fox_b_f, sg_w, sg_b, sg_norm_g, w_out):
    B, S, _ = h.shape
    p = h @ w_in
    u, v, q, k, vv, f = jnp.split(
        p, [SG_WIDTH, 2 * SG_WIDTH, 2 * SG_WIDTH + FOX_WIDTH,
            2 * SG_WIDTH + 2 * FOX_WIDTH, 2 * SG_WIDTH + 3 * FOX_WIDTH], axis=-1)
    c_out = chunked_spatial_gate(jax.nn.gelu(u), jax.nn.gelu(v), sg_w, sg_b, sg_norm_g)
    log_f = jax.nn.log_sigmoid(f.astype(jnp.float32) + fox_b_f.astype(jnp.float32))
    d_out = forgetting_attention(split_heads(q, FOX_HEADS), split_heads(k, FOX_HEADS),
                                 split_heads(vv, FOX_HEADS), log_f).reshape(B, S, FOX_WIDTH)
    return jnp.concatenate([c_out, d_out], axis=-1) @ w_out


def conv_ffn(h, w_up, conv_w, w_down):
    a = causal_dwconv(h @ w_up, conv_w)
    gate, up = jnp.split(a, 2, axis=-1)
    return (jax.nn.silu(gate) * up) @ w_down


def setup_inputs(seed: int = 0) -> dict:
    key = jax.random.key(seed)
    ks = iter(jax.random.split(key, 32))
    f32 = jnp.float32

    def w(shape, fan_in):
        return jax.random.normal(next(ks), shape, f32) * (fan_in ** -0.5)

    def gain(n):
        return 1.0 + 0.02 * jax.random.normal(next(ks), (n,), f32)

    inp = {}
    inp["x"] = jax.random.normal(next(ks), (BATCH, SEQ, D_MODEL), f32)
    inp["l0_mix_norm_g"] = gain(D_MODEL)
    inp["l0_w_in"] = w((D_MODEL, IN_AB), D_MODEL)
    inp["l0_sc_conv_w"] = w((CONV_WIDTH, SC_WIDTH), CONV_WIDTH)
    inp["l0_w_out"] = w((MIX_WIDTH, D_MODEL), MIX_WIDTH)
    inp["l0_ffn_norm_g"] = gain(D_MODEL)
    inp["l0_ffn_up"] = w((D_MODEL, 2 * D_FF), D_MODEL)
    inp["l0_ffn_conv_w"] = w((CONV_WIDTH, 2 * D_FF), CONV_WIDTH)
    inp["l0_ffn_down"] = w((D_FF, D_MODEL), D_FF)
    inp["l1_mix_norm_g"] = gain(D_MODEL)
    inp["l1_w_in"] = w((D_MODEL, IN_CD), D_MODEL)
    inp["l1_fox_b_f"] = 1.0 + 0.5 * jax.random.normal(next(ks), (FOX_HEADS,), f32)
    inp["l1_sg_w"] = w((SG_GROUPS, CHUNK, CHUNK), CHUNK)
    inp["l1_sg_b"] = 1.0 + 0.1 * jax.random.normal(next(ks), (SG_GROUPS, CHUNK), f32)
    inp["l1_sg_norm_g"] = gain(SG_WIDTH)
    inp["l1_w_out"] = w((MIX_WIDTH, D_MODEL), MIX_WIDTH)
    inp["l1_ffn_norm_g"] = gain(D_MODEL)
    inp["l1_ffn_up"] = w((D_MODEL, 2 * D_FF), D_MODEL)
    inp["l1_ffn_conv_w"] = w((CONV_WIDTH, 2 * D_FF), CONV_WIDTH)
    inp["l1_ffn_down"] = w((D_FF, D_MODEL), D_FF)
    inp["final_norm_g"] = gain(D_MODEL)
    return inp


def reference(x, l0_mix_norm_g, l0_w_in, l0_sc_conv_w, l0_w_out, l0_ffn_norm_g,
              l0_ffn_up, l0_ffn_conv_w, l0_ffn_down,
              l1_mix_norm_g, l1_w_in, l1_fox_b_f, l1_sg_w, l1_sg_b, l1_sg_norm_g,
              l1_w_out, l1_ffn_norm_g, l1_ffn_up, l1_ffn_conv_w, l1_ffn_down,
              final_norm_g):
    layers = [
        (l0_mix_norm_g,
         functools.partial(mixer_ab, w_in=l0_w_in, sc_conv_w=l0_sc_conv_w, w_out=l0_w_out),
         l0_ffn_norm_g, l0_ffn_up, l0_ffn_conv_w, l0_ffn_down),
        (l1_mix_norm_g,
         functools.partial(mixer_cd, w_in=l1_w_in, fox_b_f=l1_fox_b_f, sg_w=l1_sg_w,
                           sg_b=l1_sg_b, sg_norm_g=l1_sg_norm_g, w_out=l1_w_out),
         l1_ffn_norm_g, l1_ffn_up, l1_ffn_conv_w, l1_ffn_down),
    ]
    for i in range(DEPTH):
        mix_g, mixer, ffn_g, w_up, conv_w, w_down = layers[i]
        x = x + mixer(rmsnorm(x, mix_g))
        x = x + conv_ffn(rmsnorm(x, ffn_g), w_up, conv_w, w_down)
    return rmsnorm(x, final_norm_g)
```

```python
import numpy as np
import concourse.bass as bass
import concourse.mybir as mybir
from concourse.bass_utils import run_bass_kernel_spmd

F32 = mybir.dt.float32
BF16 = mybir.dt.bfloat16
AF = mybir.ActivationFunctionType
ALU = mybir.AluOpType

T = 2048
D = 2048
NCH = 16
DFF = 5632
NFC = 44
EPS = 1e-6
SCALE = 128 ** -0.5


class _Op:
    __slots__ = ("eng", "fn", "deps", "is_dma", "sem", "val", "signal")

    def __init__(self, eng, fn, is_dma, sem):
        self.eng = eng
        self.fn = fn
        self.deps = []
        self.is_dma = is_dma
        self.sem = sem
        self.val = 0
        self.signal = False


class Sched:
    ENGS = ("pe", "act", "dve", "pool", "sp")

    def __init__(self, nc):
        self.nc = nc
        self.eng = {"pe": nc.tensor, "act": nc.scalar, "dve": nc.vector,
                    "pool": nc.gpsimd, "sp": nc.sync}
        self.ops = []
        self.last_write = {}
        self.readers = {}
        self.dma_counts = {}
        self.last_by_sem = {}
        self.barrier_op = None
        self.seen_barrier = set()

    def _add(self, op, reads, writes):
        for k in reads:
            if isinstance(k, tuple) and k[0] == "ps" and k not in writes and op.eng != "pe":
                writes = writes + [k]
        deps = {}
        for k in reads:
            w = self.last_write.get(k)
            if w is not None:
                deps[id(w)] = w
        for k in writes:
            w = self.last_write.get(k)
            if w is not None:
                deps[id(w)] = w
            for r in self.readers.get(k, {}).values():
                deps[id(r)] = r
        if self.barrier_op is not None and op.eng not in self.seen_barrier:
            self.seen_barrier.add(op.eng)
            deps[id(self.barrier_op)] = self.barrier_op
        for d in deps.values():
            if d is op:
                continue
            if d.eng == op.eng and not d.is_dma and not op.is_dma and op.eng == "pe":
                continue
            op.deps.append(d)
            d.signal = True
        for k in reads:
            self.readers.setdefault(k, {})[op.sem] = op
        for k in writes:
            self.last_write[k] = op
            self.readers[k] = {}
        self.last_by_sem[op.sem] = op
        self.ops.append(op)
        return op

    def op(self, eng, fn, reads=(), writes=()):
        return self._add(_Op(eng, fn, False, eng), list(reads), list(writes))

    def dma(self, queue, fn, semkey, reads=(), writes=()):
        o = _Op(queue, fn, True, ("dma", semkey))
        n = self.dma_counts.get(semkey, 0) + 1
        self.dma_counts[semkey] = n
        o.val = 16 * n
        return self._add(o, list(reads), list(writes))

    def barrier(self, fn):
        o = _Op("dve", fn, False, "dve")
        for d in self.last_by_sem.values():
            o.deps.append(d)
            d.signal = True
        self.last_by_sem = {}
        self.last_by_sem[o.sem] = o
        self.ops.append(o)
        self.barrier_op = o
        self.seen_barrier = {"dve"}
        o.signal = True
        return o

    def emit(self, final_wait_eng="sp"):
        nc = self.nc
        cnt = {e: 0 for e in self.ENGS}
        for o in self.ops:
            if not o.is_dma and o.signal:
                cnt[o.eng] += 1
                o.val = cnt[o.eng]
        sems = {}

        def getsem(name):
            if name not in sems:
                sems[name] = nc.alloc_semaphore(name="s%d" % len(sems))
            return sems[name]

        waited = {}
        nwait = 0
        for o in self.ops:
            e = self.eng[o.eng]
            need = {}
            for d in o.deps:
                if d.val > need.get(d.sem, 0):
                    need[d.sem] = d.val
            for s, v in need.items():
                if waited.get((o.eng, s), 0) < v:
                    e.wait_ge(getsem(s), v)
                    waited[(o.eng, s)] = v
                    nwait += 1
            inst = o.fn()
            if o.is_dma:
                inst.then_inc(getsem(o.sem), 16)
            elif o.signal:
                inst.then_inc(getsem(o.sem), 1)
        e = self.eng[final_wait_eng]
        for k, n in self.dma_counts.items():
            e.wait_ge(getsem(("dma", k)), 16 * n)
        return dict(n_ops=len(self.ops), n_wait=nwait, n_sems=len(sems), counts=cnt)


def _blk(W):
    K, N = W.shape
    return np.ascontiguousarray(
        W.reshape(K // 128, 128, N // 128, 128).transpose(2, 1, 0, 3).reshape(N // 128, 128, K))


def _pc(v):
    return np.ascontiguousarray(v.reshape(-1, 128).T)


C_G = 0
C_SCW = 80
C_FCW0 = 104
C_FCW1 = 368
C_FOXB = 632
C_ONES = 760
C_TRI = 888
C_SEL = 1016
NCST = 1144
M_MSB = 0
M_BSG = 2048
M_SGG = 3072
M_WST = 4096
NCM = 5120


class WStream:
    def __init__(self, B, items, la=2):
        self.B, self.items, self.la, self.issued, self.tiles = B, items, la, 0, {}

    def get(self, i):
        while self.issued <= min(i + self.la, len(self.items) - 1):
            src, ncols, eng = self.items[self.issued]
            self.tiles[self.issued] = self.B.load_w(src, ncols, eng)
            self.issued += 1
        return self.tiles.pop(i)


class Builder:
    def __init__(self, debug=False):
        self.debug = debug
        nc = self.nc = bass.Bass("TRN2", target_bir_lowering=False)
        self.S = Sched(nc)
        dt = nc.dram_tensor
        self.xT = dt("xT", [D, T], F32, kind="ExternalInput").ap()
        self.cst_d = dt("cst", [128, NCST], F32, kind="ExternalInput").ap()
        self.cm_d = dt("cm", [128, NCM], F32, kind="ExternalInput").ap()
        self.cmat_d = dt("cmat", [128, 512], F32, kind="ExternalInput").ap()
        self.win = [dt("l0_win", [48, 128, D], F32, kind="ExternalInput").ap(),
                    dt("l1_win", [40, 128, D], F32, kind="ExternalInput").ap()]
        self.wf_d = dt("l1_wf", [128, 128], F32, kind="ExternalInput").ap()
        self.wout = [dt("l%d_wout" % l, [16, 128, D], F32, kind="ExternalInput").ap() for l in range(2)]
        self.wup = [dt("l%d_up" % l, [88, 128, D], F32, kind="ExternalInput").ap() for l in range(2)]
        self.wdn = [dt("l%d_dn" % l, [16, 128, DFF], F32, kind="ExternalInput").ap() for l in range(2)]
        self.y = dt("y", [D, T], F32, kind="ExternalOutput").ap()
        self.xres = dt("xres", [D, T], F32, kind="ExternalOutput" if debug else "Internal").ap()
        if debug:
            self.dbg_h = dt("dbg_h", [128, NCH * T], BF16, kind="ExternalOutput").ap()
            self.dbg_c = dt("dbg_c", [128, 45056], BF16, kind="ExternalOutput").ap()

        a = nc.alloc_sbuf_tensor
        self.hT_raw = a("hT", [128, NCH * T], BF16)
        self.hT = self.hT_raw[:, :].rearrange("p (c t) -> p c t", c=NCH)
        self.hTf = self.hT_raw[:, 0:NCH * 1024].rearrange("p (c t) -> p c t", c=NCH)
        self.hv = self.hT
        self.big = a("big", [128, 45056], BF16)
        self.cat = self.big[:, 0:NCH * T].rearrange("p (c t) -> p c t", c=NCH)
        self.wst = [a("wst%d" % i, [128, 2048], F32) for i in range(2)]
        self.wbf = [a("wbf%d" % i, [128, 2048], BF16) for i in range(3)]
        self.cst = a("cstt", [128, NCST], F32)
        self.cbf = a("cbf", [128, 6 * 128], BF16)
        self.xadd_raw = a("xaddr", [128, 2048], F32)
        self.xadd = [self.xadd_raw[:, i * 512:(i + 1) * 512] for i in range(4)]
        self.scr = a("scr", [128, 6656], BF16)
        self.ps = nc.alloc_psum_tensor("ps", [128, 4096], F32)
        self.wcount = 0
        self.xcount = 0
        self.dummy = a("dmy", [128, 8], F32)

    def bank(self, b, n=512, off=0):
        return self.ps[:, b * 512 + off: b * 512 + off + n]

    def cc(self, col, n=1):
        return self.cst[:, col:col + n]

    def barrier(self):
        nc = self.nc
        d = self.dummy
        self.S.barrier(lambda: nc.vector.memset(d[:, 0:2], 0.0))

    def load_w(self, src, ncols, cast_eng="pool"):
        nc, S = self.nc, self.S
        i = self.wcount
        self.wcount += 1
        st, bf = i % 2, i % 3
        stt, bft = self.wst[st], self.wbf[bf]
        S.dma("sp", lambda: nc.sync.dma_start(out=stt[:, 0:ncols], in_=src), ("ws", st),
              writes=[("ws", st)])
        eng = {"pool": nc.gpsimd, "act": nc.scalar, "dve": nc.vector}[cast_eng]
        if cast_eng == "act":
            fn = lambda: nc.scalar.copy(out=bft[:, 0:ncols], in_=stt[:, 0:ncols])
        else:
            fn = lambda: eng.tensor_copy(out=bft[:, 0:ncols], in_=stt[:, 0:ncols])
        S.op(cast_eng, fn, reads=[("ws", st)], writes=[("wb", bf)])
        return bft, ("wb", bf)

    def mm(self, out, lhsT, rhs, start, stop, reads, writes, skip=False):
        nc = self.nc
        if skip:
            fn = lambda: nc.tensor.matmul(out, lhsT=lhsT, rhs=rhs, start=start, stop=stop,
                                          skip_group_check=True)
        else:
            fn = lambda: nc.tensor.matmul(out, lhsT=lhsT, rhs=rhs, start=start, stop=stop)
        self.S.op("pe", fn, reads=reads, writes=writes)

    def act(self, out, in_, func, reads, writes, scale=1.0, bias=None, accum=None):
        nc = self.nc
        kw = {}
        if bias is not None:
            kw["bias"] = bias
        if accum is not None:
            kw["accum_out"] = accum
        self.S.op("act", lambda: nc.scalar.activation(out=out, in_=in_, func=func, scale=scale, **kw),
                  reads=reads, writes=writes)

    def tt(self, eng, out, in0, in1, op, reads, writes):
        e = {"dve": self.nc.vector, "pool": self.nc.gpsimd}[eng]
        self.S.op(eng, lambda: e.tensor_tensor(out=out, in0=in0, in1=in1, op=op), reads=reads, writes=writes)

    def ts(self, eng, out, in0, s1, op0, reads, writes, s2=None, op1=None):
        e = {"dve": self.nc.vector, "pool": self.nc.gpsimd}[eng]
        if op1 is None:
            fn = lambda: e.tensor_scalar(out=out, in0=in0, scalar1=s1, scalar2=None, op0=op0)
        else:
            fn = lambda: e.tensor_scalar(out=out, in0=in0, scalar1=s1, scalar2=s2, op0=op0, op1=op1)
        self.S.op(eng, fn, reads=reads, writes=writes)

    def stt(self, out, in0, scalar, in1, op0, op1, reads, writes):
        nc = self.nc
        self.S.op("dve", lambda: nc.vector.scalar_tensor_tensor(out=out, in0=in0, scalar=scalar, in1=in1,
                                                                op0=op0, op1=op1), reads=reads, writes=writes)

    def copy(self, eng, out, in_, reads, writes):
        nc = self.nc
        if eng == "act":
            fn = lambda: nc.scalar.copy(out=out, in_=in_)
        else:
            e = {"dve": nc.vector, "pool": nc.gpsimd}[eng]
            fn = lambda: e.tensor_copy(out=out, in_=in_)
        self.S.op(eng, fn, reads=reads, writes=writes)

    def dma(self, out, in_, semkey, reads, writes, queue="sp"):
        nc = self.nc
        e = nc.sync if queue == "sp" else nc.scalar
        self.S.dma(queue, lambda: e.dma_start(out=out, in_=in_), semkey, reads=reads, writes=writes)

    def init(self):
        nc, S = self.nc, self.S
        self.dma(self.cst[:, :], self.cst_d, "cst", [], ["cst"])
        for k, col in enumerate((C_ONES, C_TRI)):
            self.copy("dve", self.cbf[:, k * 128:(k + 1) * 128], self.cst[:, col:col + 128], ["cst"], ["cbf"])
        stg = self.hT_raw[:, 0:1024].bitcast(F32)
        self.dma(stg, self.cmat_d, "cmat", [], ["cmat"])
        self.copy("dve", self.cbf[:, 256:768], stg, ["cmat"], ["cbf"])
        self.ones_bf = self.cbf[:, 0:128]
        self.tri_bf = self.cbf[:, 128:256]
        self.uinc_bf = self.cbf[:, 256:384]
        self.lstr_bf = self.cbf[:, 384:512]
        self.ident_bf = self.cbf[:, 512:640]
        self.negtri_bf = self.cbf[:, 640:768]

    def norm(self, src, gcol, tgs, hcol0, final=False):
        S = self.S
        srcv = src.rearrange("(c p) t -> p c t", p=128)
        slabs = [self.big[:, k * 16384:(k + 1) * 16384].bitcast(F32).rearrange("p (c t) -> p c t", c=NCH)
                 for k in range(2)]
        sq = [self.scr[:, k * 512:(k + 1) * 512] for k in range(2)]
        std = [self.scr[:, 1024 + k * 1024: 2048 + k * 1024].bitcast(F32) for k in range(2)]
        yv = self.y.rearrange("(c p) t -> p c t", p=128)
        for k, tg in enumerate(tgs):
            sl = slabs[k % 2]
            skq = [("slab", k % 2, q) for q in range(4)]
            for q in range(4):
                self.dma(sl[:, q * 4:(q + 1) * 4, :], srcv[:, q * 4:(q + 1) * 4, tg * 512:(tg + 1) * 512],
                         ("slab", k % 2, q), [("xres", c, tg) for c in range(q * 4, q * 4 + 4)], [skq[q]])
            b = k % 2
            for c in range(NCH):
                self.act(sq[c % 2], sl[:, c, :], AF.Square, [skq[c // 4]], [("sq", c % 2)])
                self.mm(self.bank(b), self.ones_bf, sq[c % 2], c == 0, c == NCH - 1,
                        [("sq", c % 2), "cbf"], [("ps", b)])
            st = std[k % 2]
            self.act(st, self.bank(b), AF.Sqrt, [("ps", b)], [("std", k % 2)], scale=1.0 / D, bias=self.eps_ap)
            self.S.op("dve", self._recip(st, st), reads=[("std", k % 2)], writes=[("std", k % 2)])
            for c in range(NCH):
                if final:
                    self.stt(sl[:, c, :], sl[:, c, :], self.cc(gcol + c), st, ALU.mult, ALU.mult,
                             [skq[c // 4], ("std", k % 2), "cst"], [skq[c // 4]])
                else:
                    lt = hcol0 // 512 + k
                    self.stt(self.hv[:, c, lt * 512:(lt + 1) * 512], sl[:, c, :], self.cc(gcol + c), st,
                             ALU.mult, ALU.mult, [skq[c // 4], ("std", k % 2), "cst"], [("hT", c, lt)])
            if final:
                for q in range(4):
                    self.dma(yv[:, q * 4:(q + 1) * 4, tg * 512:(tg + 1) * 512], sl[:, q * 4:(q + 1) * 4, :],
                             ("yout", k % 2, q), [skq[q]], [("y", q, tg)])

    def _recip(self, out, in_):
        nc = self.nc
        return lambda: nc.vector.reciprocal(out=out, in_=in_)

    def xload(self, src, n, tg):
        j = self.xcount % 4
        self.xcount += 1
        self.dma(self.xadd[j], src[n * 128:(n + 1) * 128, tg * 512:(tg + 1) * 512], ("xl", j),
                 [("xres", n, tg)], [("xadd", j)], queue="act")
        return j

    def xadd_store(self, j, b, n, tg):
        xt = self.xadd[j]
        self.tt("dve", xt, self.bank(b), xt, ALU.add, [("ps", b), ("xadd", j)], [("xadd", j)])
        self.dma(self.xres[n * 128:(n + 1) * 128, tg * 512:(tg + 1) * 512], xt, ("xs", j),
                 [("xadd", j)], [("xres", n, tg)], queue="act")

    def out_proj(self, wd, src):
        tiles = [(n, tg) for n in range(NCH) for tg in range(4)]
        pend = {}
        for idx in range(2):
            pend[idx] = self.xload(src, *tiles[idx])
        w = None
        ws = WStream(self, [(wd[n], D, "dve") for n in range(NCH)])
        for idx, (n, tg) in enumerate(tiles):
            if tg == 0:
                w, wk = ws.get(n)
            if idx + 2 < len(tiles):
                pend[idx + 2] = self.xload(src, *tiles[idx + 2])
            b = idx % 4
            for c in range(NCH):
                self.mm(self.bank(b), w[:, c * 128:(c + 1) * 128], self.cat[:, c, tg * 512:(tg + 1) * 512],
                        c == 0, c == NCH - 1, [wk, ("cat", c, tg)], [("ps", b)])
            self.xadd_store(pend.pop(idx), b, n, tg)

    def proj_fm(self, w, wk, b, lt):
        for c in range(NCH):
            self.mm(self.bank(b), w[:, c * 128:(c + 1) * 128], self.hv[:, c, lt * 512:(lt + 1) * 512],
                    c == 0, c == NCH - 1, [wk, ("hT", c, lt)], [("ps", b)])

    def proj_tm(self, w, wk, b, slot, tt_, ncols=128, wstride=128):
        for c in range(NCH):
            self.mm(self.bank(b, ncols, slot * ncols), self.hT[:, c, tt_ * 128:(tt_ + 1) * 128],
                    w[:, c * wstride:c * wstride + ncols], c == 0, c == NCH - 1,
                    [wk, ("hT", c, tt_ // 4)], [("ps", b)])

    def qkv_bufs(self, st, vt1):
        if st == 0:
            qT = self.scr[:, 0:2048]
            kT = self.scr[:, 2048:4096]
            vt = self.scr[:, 4096:6144].rearrange("p (t d) -> p t d", t=16)
        else:
            xb = self.xadd_raw[:, :].bitcast(BF16)
            qT = xb[:, 0:2048]
            kT = xb[:, 2048:4096]
            vt = vt1.rearrange("p (t d) -> p t d", t=16)
        return qT, kT, vt

    def qkv_gen(self, wd, bq, bk, bv, bufs, st, pb=(0, 1)):
        qT, kT, vt = bufs
        for which, blk_ in (("q", bq), ("k", bk)):
            w, wk = self.load_w(wd[blk_], D)
            dst = qT if which == "q" else kT
            for tg in range(4):
                b = pb[tg % 2]
                for c in range(NCH):
                    self.mm(self.bank(b), w[:, c * 128:(c + 1) * 128], self.hv[:, c, tg * 512:(tg + 1) * 512],
                            c == 0, c == NCH - 1, [wk, ("hT", c, tg)], [("ps", b)])
                    if c % 4 == 3 and c != NCH - 1:
                        yield
                if which == "q":
                    self.act(dst[:, tg * 512:(tg + 1) * 512], self.bank(b), AF.Copy, [("ps", b)],
                             [("q", st, tg)], scale=SCALE)
                else:
                    self.copy("dve", dst[:, tg * 512:(tg + 1) * 512], self.bank(b), [("ps", b)], [("k", st, tg)])
                yield
        w, wk = self.load_w(wd[bv], D)
        for t4 in range(4):
            b = pb[t4 % 2]
            for s_ in range(4):
                self.proj_tm(w, wk, b, s_, t4 * 4 + s_)
                if s_ != 3:
                    yield
            self.copy("dve", vt[:, t4 * 4:(t4 + 1) * 4, :],
                      self.bank(b).rearrange("p (t d) -> p t d", t=4), [("ps", b)], [("v", st, t4)])
            yield

    def mixer0(self):
        S = self.S
        wd = self.win[0]
        big = self.big
        sp0 = 32768
        u_full = big[:, sp0:sp0 + 4104].bitcast(F32)
        gcsb = big[:, sp0 + 4104:sp0 + 5128].bitcast(F32)
        acc = big[:, sp0 + 5128:sp0 + 6152].bitcast(F32)
        nc = self.nc
        S.op("pool", lambda: nc.gpsimd.memset(u_full[:, 0:2], 0.0), writes=[("u", -1)])
        import os
        parts = os.environ.get("MIX0", "AB")
        for g in range(8 if "B" in parts else 0):
            wgb, kgb = self.load_w(wd[24 + g], D)
            wgc, kgc = self.load_w(wd[32 + g], D)
            whi, khi = self.load_w(wd[40 + g], D)
            for tg in range(4):
                b0 = (tg % 2) * 3
                self.proj_fm(wgb, kgb, b0, tg)
                self.proj_fm(wgc, kgc, b0 + 1, tg)
                self.proj_fm(whi, khi, b0 + 2, tg)
                self.copy("act", gcsb, self.bank(b0 + 1), [("ps", b0 + 1)], ["gcsb"])
                c0 = 2 + tg * 512
                self.tt("dve", u_full[:, c0:c0 + 512], self.bank(b0 + 2), gcsb, ALU.mult,
                        [("ps", b0 + 2), "gcsb"], [("u", tg)])
                wc = C_SCW + g * 3
                self.ts("dve", acc, u_full[:, c0:c0 + 512], self.cc(wc + 2), ALU.mult, [("u", tg), "cst"], ["acc"])
                self.stt(acc, u_full[:, c0 - 1:c0 + 511], self.cc(wc + 1), acc, ALU.mult, ALU.add,
                         [("u", tg), ("u", tg - 1), "acc", "cst"], ["acc"])
                self.stt(acc, u_full[:, c0 - 2:c0 + 510], self.cc(wc), acc, ALU.mult, ALU.add,
                         [("u", tg), ("u", tg - 1), "acc", "cst"], ["acc"])
                self.tt("dve", self.cat[:, 8 + g, tg * 512:(tg + 1) * 512], self.bank(b0), acc, ALU.mult,
                        [("ps", b0), "acc"], [("cat", 8 + g, tg)])
        self.barrier()
        msk = big[:, sp0:sp0 + 2048]
        mst = big[:, sp0 + 2048:sp0 + 6144].bitcast(F32)
        self.dma(mst, self.cm_d[:, M_MSB:M_MSB + 2048], "cmm", [], ["mst"])
        self.copy("dve", msk, mst, ["mst"], ["msk"])
        self.barrier()
        o = sp0 + 2048
        zsb = [[big[:, o + (2 * sl + k) * 1024:o + (2 * sl + k + 1) * 1024].bitcast(F32) for k in range(2)]
               for sl in range(2)]
        o += 4096
        esb = [big[:, o + k * 1024:o + (k + 1) * 1024].bitcast(F32) for k in range(1)]
        o += 1024
        spb = [[big[:, o + (2 * sl + k) * 512:o + (2 * sl + k + 1) * 512] for k in range(2)] for sl in range(2)]
        o += 2048
        abf = [[big[:, o + sl * 512:o + (sl + 1) * 512]] * 2 for sl in range(2)]
        o += 1024
        vt1 = big[:, o:o + 2048]
        o += 2048
        assert o <= 45056
        bufsets = [self.qkv_bufs(0, None), self.qkv_bufs(1, vt1)]
        self.zcnt = 0
        self.ecnt = 0

        def sb_stream(h, G, sl, st, accb, ob):
            qT, kT, vt = bufsets[st]
            blocks = list(range(4 * G + 3, -1, -1))
            qs = qT[:, G * 512:(G + 1) * 512]
            nb = len(blocks)
            zbank = {}

            def front(i, cn):
                zb = 2 + self.zcnt % 2
                self.zcnt += 1
                e = esb[0]
                r = i - 4 * G
                self.mm(self.bank(zb), kT[:, i * 128:(i + 1) * 128], qs, True, r < 0,
                        [("k", st, i // 4), ("q", st, G)], [("ps", zb)])
                if r >= 0:
                    self.mm(self.bank(zb), self.ident_bf, msk[:, r * 512:(r + 1) * 512], False, True,
                            ["cbf", "msk"], [("ps", zb)])
                self.act(e, self.bank(zb), AF.Exp, [("ps", zb)], ["esb"])
                self.act(spb[sl][cn % 2], e, AF.Ln, ["esb"], [("sp", sl, cn % 2)], bias=self.one_ap)
                zbank[cn % 2] = zb

            def zc(cn):
                zb = zbank[cn % 2]
                self.copy("dve", zsb[sl][cn % 2], self.bank(zb), [("ps", zb)], [("zsb", sl, cn % 2)])

            def back_u(i, cn, first, last):
                self.mm(self.bank(accb), self.uinc_bf, spb[sl][cn % 2], first, False,
                        [("sp", sl, cn % 2), "cbf"], [("ps", accb)], skip=True)

            def back_b(i, cn, first, last):
                self.tt("dve", zsb[sl][cn % 2], zsb[sl][cn % 2], self.bank(accb), ALU.subtract,
                        [("zsb", sl, cn % 2), ("ps", accb)], [("zsb", sl, cn % 2)])
                if not last:
                    self.mm(self.bank(accb), self.lstr_bf, spb[sl][cn % 2], False, False,
                            [("sp", sl, cn % 2), "cbf"], [("ps", accb)], skip=True)
                self.act(abf[sl][cn % 2], zsb[sl][cn % 2], AF.Exp, [("zsb", sl, cn % 2)], [("abf", sl)])

            def back_av(i, cn, first, last):
                self.mm(self.bank(ob), vt[:, i, :], abf[sl][cn % 2], first, last,
                        [("v", st, i // 4), ("abf", sl)], [("ps", ob)])

            front(blocks[0], 0)
            yield
            zc(0)
            yield
            for k in range(nb):
                if k + 1 < nb:
                    front(blocks[k + 1], k + 1)
                    yield
                args = (blocks[k], k, k == 0, k == nb - 1)
                back_u(*args)
                yield
                back_b(*args)
                yield
                if k + 1 < nb:
                    zc(k + 1)
                back_av(*args)
                yield
            self.copy("dve", self.cat[:, h, G * 512:(G + 1) * 512], self.bank(ob), [("ps", ob)],
                      [("cat", h, G)])

        nh = int(os.environ.get("NH", "8")) if "A" in parts else 0
        if nh:
            for _ in self.qkv_gen(wd, 0, 8, 16, bufsets[0], 0):
                pass
        for h in range(nh):
            st = h % 2
            pg = self.qkv_gen(wd, h + 1, 9 + h, 17 + h, bufsets[1 - st], 1 - st) if h + 1 < nh else iter(())
            todo = [3, 2, 1, 0]
            free_slots = [(0, 4, 6), (1, 5, 7)]
            active = []
            steps = 0
            while todo or active:
                while todo and free_slots:
                    sl, accb, ob = free_slots.pop(0)
                    active.append((sb_stream(h, todo.pop(0), sl, st, accb, ob), (sl, accb, ob)))
                for item in list(active):
                    g, slot = item
                    try:
                        next(g)
                    except StopIteration:
                        active.remove(item)
                        free_slots.append(slot)
                    steps += 1
                    if steps % 3 == 0:
                        next(pg, None)
            for _ in pg:
                pass
        self.barrier()

    def ffn(self, layer, gcol):
        S = self.S
        wup, wdn = self.wup[layer], self.wdn[layer]
        fcw = C_FCW0 if layer == 0 else C_FCW1
        gT = self.big[:, 0:NFC * 1024].rearrange("p (f t) -> p f t", f=NFC)
        h2 = self.hT_raw[:, 16384:32768]
        cbuf = {}
        for k in range(2):
            cbuf[("g", k)] = h2[:, k * 4096:k * 4096 + 2048].bitcast(F32)
            cbuf[("u", k)] = h2[:, k * 4096 + 2048:k * 4096 + 4096].bitcast(F32)
        halo = h2[:, 8192:8192 + 352].bitcast(F32)
        self.hv = self.hTf
        for half in range(2):
            self.norm(self.xres, gcol, [half * 2, half * 2 + 1], 0)
            self.barrier()
            ws = WStream(self, [(wup[(jj // 2) + NFC * (jj % 2)], D, "act") for jj in range(2 * NFC)])
            for j in range(NFC):
                k = j % 2
                for which, blk_, b0 in (("g", j, 4 * k), ("u", NFC + j, 4 * k + 2)):
                    w, wk = ws.get(2 * j + (0 if which == "g" else 1))
                    for lt in range(2):
                        self.proj_fm(w, wk, b0 + lt, lt)
                    P = self.ps[:, b0 * 512:b0 * 512 + 1024]
                    pk = [("ps", b0), ("ps", b0 + 1)]
                    cb = cbuf[(which, k)]
                    ck = ("cb", which, k)
                    wc = fcw + blk_ * 3
                    self.act(cb, P, AF.Identity, pk, [ck], scale=self.cc(wc + 2))
                    self.stt(cb[:, 1:1024], P[:, 0:1023], self.cc(wc + 1), cb[:, 1:1024], ALU.mult, ALU.add,
                             pk + [ck, "cst"], [ck])
                    self.stt(cb[:, 2:1024], P[:, 0:1022], self.cc(wc), cb[:, 2:1024], ALU.mult, ALU.add,
                             pk + [ck, "cst"], [ck])
                    hl = halo[:, blk_ * 2:blk_ * 2 + 2]
                    if half == 0:
                        self.copy("act", hl, P[:, 1022:1024], pk, [("halo", blk_)])
                    else:
                        hk = ("halo", blk_)
                        self.stt(cb[:, 0:1], hl[:, 1:2], self.cc(wc + 1), cb[:, 0:1], ALU.mult, ALU.add,
                                 [hk, ck, "cst"], [ck])
                        self.stt(cb[:, 0:1], hl[:, 0:1], self.cc(wc), cb[:, 0:1], ALU.mult, ALU.add,
                                 [hk, ck, "cst"], [ck])
                        self.stt(cb[:, 1:2], hl[:, 1:2], self.cc(wc), cb[:, 1:2], ALU.mult, ALU.add,
                                 [hk, ck, "cst"], [ck])
                cg, cu = cbuf[("g", k)], cbuf[("u", k)]
                self.act(cg, cg, AF.Silu, [("cb", "g", k)], [("cb", "g", k)])
                self.tt("pool", gT[:, j, :], cg, cu, ALU.mult, [("cb", "g", k), ("cb", "u", k)], [("gT", j)])
            tiles = [(n, lt) for n in range(NCH) for lt in range(2)]
            pend = {}
            for idx in range(2):
                n, lt = tiles[idx]
                pend[idx] = self.xload(self.xres, n, half * 2 + lt)
            ws = WStream(self, [(wdn[nn // 4][:, (nn % 4) * 1408:(nn % 4 + 1) * 1408], 1408, "dve")
                                for nn in range(4 * NCH)])
            for n in range(NCH):
                b0 = (n % 4) * 2
                for piece in range(4):
                    w, wk = ws.get(n * 4 + piece)
                    for lt in range(2):
                        for fl in range(11):
                            fc = piece * 11 + fl
                            self.mm(self.bank(b0 + lt), w[:, fl * 128:(fl + 1) * 128],
                                    gT[:, fc, lt * 512:(lt + 1) * 512], fc == 0, fc == NFC - 1,
                                    [wk, ("gT", fc)], [("ps", b0 + lt)])
                for lt in range(2):
                    idx = n * 2 + lt
                    if idx + 2 < len(tiles):
                        n2, lt2 = tiles[idx + 2]
                        pend[idx + 2] = self.xload(self.xres, n2, half * 2 + lt2)
                    self.xadd_store(pend.pop(idx), b0 + lt, n, half * 2 + lt)
            self.barrier()
        self.hv = self.hT

    def mixer1(self):
        S, nc = self.S, self.nc
        wd = self.win[1]
        big = self.big
        sp0 = 32768
        bsg = big[:, sp0:sp0 + 2048].bitcast(F32)
        sgg = big[:, sp0 + 2048:sp0 + 3072]
        wst = big[:, sp0 + 3072:sp0 + 4096]
        o = sp0 + 4096
        stg = big[:, o:o + 2048].bitcast(F32)
        o += 2048
        ug = big[:, o:o + 2048]
        o += 2048
        tmp = [big[:, o + k * 1024:o + (k + 1) * 1024].bitcast(F32) for k in range(2)]
        o += 2048
        small = big[:, o:o + 2048].bitcast(F32)
        gv = big[:, 8 * T:16 * T].rearrange("p (t c) -> p t c", t=16)
        GV = "gv"
        self.dma(bsg, self.cm_d[:, M_BSG:M_BSG + 1024], "cm0", [], ["bsg"])
        self.dma(stg, self.cm_d[:, M_SGG:M_SGG + 1024], "cm1", [], ["stg"])
        self.copy("dve", sgg, stg, ["stg"], ["sgg"])
        self.dma(stg, self.cm_d[:, M_WST:M_WST + 1024], "cm2", [], ["stg"])
        for g in range(8):
            self.tt("dve", wst[:, g * 128:(g + 1) * 128], stg[:, g * 128:(g + 1) * 128],
                    self.cc(C_TRI, 128), ALU.mult, ["stg", "cst"], ["wst"])
        for bb in range(8):
            w, wk = self.load_w(wd[8 + bb], D)
            for t4 in range(4):
                b = 2 + t4 % 2
                for s in range(4):
                    self.proj_tm(w, wk, b, s, t4 * 4 + s)
                self.act(gv[:, t4 * 4:(t4 + 1) * 4, bb * 128:(bb + 1) * 128],
                         self.bank(b).rearrange("p (t d) -> p t d", t=4), AF.Gelu, [("ps", b)], [GV])
        s1 = small[:, 0:16]
        s2 = small[:, 16:32]
        mean = small[:, 32:48]
        msq = small[:, 48:64]
        var = small[:, 64:80]
        junk = stg.bitcast(BF16)[:, 0:1024]
        for tt_ in range(16):
            self.act(junk, gv[:, tt_, :], AF.Identity, [GV], ["junk"], accum=s1[:, tt_:tt_ + 1])
            self.act(junk, gv[:, tt_, :], AF.Square, [GV], ["junk"], accum=s2[:, tt_:tt_ + 1])
        self.ts("dve", mean, s1, 1.0 / 1024, ALU.mult, ["junk"], ["mean"])
        self.tt("dve", msq, mean, mean, ALU.mult, ["mean"], ["msq"])
        self.stt(var, s2, 1.0 / 1024, msq, ALU.mult, ALU.subtract, ["junk", "msq"], ["var"])
        self.act(var, var, AF.Sqrt, ["var"], ["var"], bias=self.eps_ap)
        self.S.op("dve", self._recip(var, var), reads=["var"], writes=["var"])
        for tt_ in range(16):
            self.ts("dve", gv[:, tt_, :], gv[:, tt_, :], mean[:, tt_:tt_ + 1], ALU.subtract, [GV, "mean", "var"], [GV],
                    s2=var[:, tt_:tt_ + 1], op1=ALU.mult)
            self.tt("dve", gv[:, tt_, :], gv[:, tt_, :], sgg, ALU.mult, [GV, "sgg"], [GV])
        for g in range(8):
            w, wk = self.load_w(wd[g], D)
            for tg in range(4):
                b = tg % 2
                self.proj_fm(w, wk, b, tg)
                self.act(ug[:, tg * 512:(tg + 1) * 512], self.bank(b), AF.Gelu, [("ps", b)], [("ug", tg)])
            for tg in range(4):
                b = 4 + tg % 2
                for s in range(4):
                    n = tg * 4 + s
                    self.mm(self.bank(b, 128, s * 128), gv[:, n, g * 128:(g + 1) * 128],
                            wst[:, g * 128:(g + 1) * 128], True, True, [GV, "wst"], [("ps", b)])
                tm = tmp[tg % 2]
                self.tt("dve", tm.rearrange("p (s t) -> p s t", s=4),
                        self.bank(b).rearrange("p (s t) -> p s t", s=4),
                        bsg[:, g * 128:(g + 1) * 128].unsqueeze(1).to_broadcast([128, 4, 128]), ALU.add,
                        [("ps", b), "bsg"], [("tmp", tg % 2)])
                self.tt("dve", self.cat[:, g, tg * 512:(tg + 1) * 512], tm, ug[:, tg * 512:(tg + 1) * 512],
                        ALU.mult, [("tmp", tg % 2), ("ug", tg)], [("cat", g, tg)])
        self.barrier()
        xf = small[:, 128:256]
        lf = small[:, 256:384]
        pfx = small[:, 384:512]
        negc = small[:, 512:640]
        rs = small[:, 640:768]
        bt = small[:, 768:1024]
        wf, wfk = self.load_w(self.wf_d, 128, cast_eng="dve")
        for tt_ in range(16):
            self.proj_tm(wf, wfk, 0, tt_, tt_, ncols=8, wstride=8)
        self.tt("dve", xf, self.bank(0, 128), self.cc(C_FOXB, 128), ALU.add, [("ps", 0), "cst"], ["xf"])
        self.act(xf, xf, AF.Exp, ["xf"], ["xf"], scale=-1.0)
        self.act(lf, xf, AF.Ln, ["xf"], ["lf"], bias=self.one_ap)
        self.mm(self.bank(1, 128), self.cc(C_TRI, 128), lf, True, True, ["lf", "cst"], [("ps", 1)])
        self.mm(self.bank(2, 128), self.cc(C_ONES, 128), lf, True, True, ["lf", "cst"], [("ps", 2)])
        S.op("dve", lambda: nc.vector.memset(pfx[:, 0:8], 0.0), writes=["pfx"])
        for tt_ in range(1, 16):
            self.tt("dve", pfx[:, tt_ * 8:(tt_ + 1) * 8], pfx[:, (tt_ - 1) * 8:tt_ * 8],
                    self.bank(2, 8, (tt_ - 1) * 8), ALU.add, ["pfx", ("ps", 2)], ["pfx"])
        self.tt("dve", negc, pfx, self.bank(1, 128), ALU.add, ["pfx", ("ps", 1)], ["negc"])
        self.mm(self.bank(3, 128), self.cc(C_SEL, 128), negc, True, True, ["negc", "cst"], [("ps", 3)])
        self.copy("dve", rs, self.bank(3, 128), [("ps", 3)], ["rs"])
        self.barrier()
        pbf = [stg.bitcast(BF16)[:, k * 128:(k + 1) * 128] for k in range(8)]
        rden = tmp[0][:, 0:128]
        rsv = rs.rearrange("p (t h) -> p t h", h=8)
        bts = [bt, small[:, 128:384]]
        vt1 = big[:, sp0:sp0 + 2048]
        bufsets = [self.qkv_bufs(0, None), self.qkv_bufs(1, vt1)]
        blocks = [(tb, i) for tb in range(16) for i in range(tb + 1)]
        LA = 4

        def fox_gen(h, st):
            qT, kT, vt = bufsets[st]
            btt = bts[h % 2]
            btk = ("bt", h % 2)
            for i in range(16):
                self.ts("dve", btt[:, i * 16:(i + 1) * 16], rsv[:, :, h], -1.0, ALU.mult, ["rs", "negc"], [btk],
                        s2=negc[:, i * 8 + h:i * 8 + h + 1], op1=ALU.add)
            yield

            def zstep(n):
                tb, i = blocks[n]
                zb = 1 + n % 5
                p, pk = pbf[n % 8], ("pbf", n % 8)
                self.mm(self.bank(zb, 128), kT[:, i * 128:(i + 1) * 128], qT[:, tb * 128:(tb + 1) * 128],
                        True, i != tb, [("k", st, i // 4), ("q", st, tb // 4)], [("ps", zb)])
                if i == tb:
                    self.mm(self.bank(zb, 128), self.ident_bf, self.negtri_bf, False, True, ["cbf"], [("ps", zb)])
                self.act(p, self.bank(zb, 128), AF.Exp, [("ps", zb), btk], [pk],
                         bias=btt[:, i * 16 + tb:i * 16 + tb + 1])

            def ostep(n):
                tb, i = blocks[n]
                ob = 6 + tb % 2
                p, pk = pbf[n % 8], ("pbf", n % 8)
                self.mm(self.bank(ob, 128), vt[:, i, :], p, i == 0, i == tb, [("v", st, i // 4), pk],
                        [("ps", ob)], skip=True)
                self.mm(self.bank(ob, 128, 128), self.ones_bf, p, False, i == tb, [pk, "cbf"], [("ps", ob)],
                        skip=True)
                if i == tb:
                    self.S.op("dve", self._recip(rden, self.bank(ob, 128, 128)), reads=[("ps", ob)],
                              writes=["rden"])
                    self.tt("dve", self.cat[:, 8 + h, tb * 128:(tb + 1) * 128], self.bank(ob, 128), rden,
                            ALU.mult, [("ps", ob), "rden"], [("cat", 8 + h, tb // 4), GV])

            for n in range(len(blocks) + LA):
                if n < len(blocks):
                    zstep(n)
                if n - LA >= 0:
                    ostep(n - LA)
                yield

        for _ in self.qkv_gen(wd, 16, 24, 32, bufsets[0], 0, pb=(0, 0)):
            pass
        for h in range(8):
            st = h % 2
            pg = self.qkv_gen(wd, 17 + h, 25 + h, 33 + h, bufsets[1 - st], 1 - st, pb=(0, 0)) if h + 1 < 8 else iter(())
            steps = 0
            for _ in fox_gen(h, st):
                steps += 1
                if steps % 3 == 0:
                    next(pg, None)
            for _ in pg:
                pass
        self.barrier()

    def build(self, upto=99):
        nc, S = self.nc, self.S
        self.init()
        S.op("dve", lambda: nc.vector.memset(self.dummy[:, 2:3], EPS), writes=["dummy"])
        S.op("dve", lambda: nc.vector.memset(self.dummy[:, 3:4], 1.0), writes=["dummy"])
        self.eps_ap = self.dummy[:, 2:3]
        self.one_ap = self.dummy[:, 3:4]
        self.barrier()
        steps = [
            lambda: self.norm(self.xT, C_G + 0, [0, 1, 2, 3], 0),
            lambda: self.mixer0(),
            lambda: self.out_proj(self.wout[0], self.xT),
            lambda: self.ffn(0, C_G + 16),
            lambda: self.norm(self.xres, C_G + 32, [0, 1, 2, 3], 0),
            lambda: self.mixer1(),
            lambda: self.out_proj(self.wout[1], self.xres),
            lambda: self.ffn(1, C_G + 48),
            lambda: self.norm(self.xres, C_G + 64, [0, 1, 2, 3], 0, final=True),
        ]
        for i, st in enumerate(steps):
            if i > upto:
                break
            st()
            self.barrier()
        if self.debug:
            self.dma(self.dbg_h, self.hT_raw[:, :], "dbgh", [], [])
            self.dma(self.dbg_c, self.big[:, :], "dbgc", [], [])
        info = S.emit()
        return nc, info


_CACHE = {}


def _prep_shared(inp):
    f = lambda k: np.asarray(inp[k], dtype=np.float32)
    cst = np.zeros((128, NCST), np.float32)
    for i, k in enumerate(("l0_mix_norm_g", "l0_ffn_norm_g", "l1_mix_norm_g", "l1_ffn_norm_g", "final_norm_g")):
        cst[:, C_G + 16 * i:C_G + 16 * (i + 1)] = _pc(f(k))
    scw = f("l0_sc_conv_w")
    cst[:, C_SCW:C_SCW + 24] = scw.reshape(3, 8, 128).transpose(2, 1, 0).reshape(128, 24)
    for col, k in ((C_FCW0, "l0_ffn_conv_w"), (C_FCW1, "l1_ffn_conv_w")):
        cw = f(k)
        cst[:, col:col + 264] = cw.reshape(3, 88, 128).transpose(2, 1, 0).reshape(128, 264)
    cst[:, C_FOXB:C_FOXB + 128] = np.tile(f("l1_fox_b_f"), 16)[None, :]
    r = np.arange(128)
    cst[:, C_ONES:C_ONES + 128] = 1.0
    cst[:, C_TRI:C_TRI + 128] = (r[:, None] <= r[None, :])
    cst[:, C_SEL:C_SEL + 128] = (r[:, None] == 64)
    cmat = np.zeros((128, 512), np.float32)
    cmat[:, 0:128] = (r[:, None] >= r[None, :])
    cmat[:, 128:256] = (r[:, None] < r[None, :])
    cmat[:, 256:384] = (r[:, None] == r[None, :])
    cmat[:, 384:512] = np.where(r[:, None] <= r[None, :], 0.0, -30000.0)
    cm = np.zeros((128, NCM), np.float32)
    tcol = np.arange(512)
    for q in range(4):
        cm[:, M_MSB + q * 512:M_MSB + (q + 1) * 512] = np.where((128 * q + r[:, None]) < tcol[None, :], 0.0, -30000.0)
    cm[:, M_BSG:M_BSG + 1024] = f("l1_sg_b").reshape(1, 1024)
    cm[:, M_SGG:M_SGG + 1024] = f("l1_sg_norm_g").reshape(1, 1024)
    cm[:, M_WST:M_WST + 1024] = f("l1_sg_w").transpose(2, 0, 1).reshape(128, 1024)
    w1 = f("l1_w_in")
    sh = {
        "cst": cst, "cm": cm, "cmat": cmat,
        "l0_win": _blk(f("l0_w_in")), "l1_win": _blk(w1[:, :5120]),
        "l1_wf": np.ascontiguousarray(w1[:, 5120:5128].reshape(16, 128, 8).transpose(1, 0, 2).reshape(128, 128)),
        "l0_wout": _blk(f("l0_w_out")), "l1_wout": _blk(f("l1_w_out")),
        "l0_up": _blk(f("l0_ffn_up")), "l1_up": _blk(f("l1_ffn_up")),
        "l0_dn": _blk(f("l0_ffn_down")), "l1_dn": _blk(f("l1_ffn_down")),
    }
    return sh


def kernel(**inputs):
    x = np.asarray(inputs["x"], dtype=np.float32)
    if "nc" not in _CACHE:
        _CACHE["nc"], _CACHE["info"] = Builder().build()
    nc = _CACHE["nc"]
    sh = _prep_shared(inputs)
    in_maps = []
    for b in range(8):
        m = dict(sh)
        m["xT"] = np.ascontiguousarray(x[b].T)
        in_maps.append(m)
    res = run_bass_kernel_spmd(nc, in_maps, core_ids=list(range(8)))
    out = np.stack([np.ascontiguousarray(res.results[b]["y"].T) for b in range(8)], axis=0)
    return out.astype(np.float32)
```

```python
import numpy as np
import concourse.bass as bass
import concourse.mybir as mybir
from concourse.bass_utils import run_bass_kernel_spmd

F32 = mybir.dt.float32
BF16 = mybir.dt.bfloat16
AF = mybir.ActivationFunctionType
ALU = mybir.AluOpType

T = 2048
D = 2048
NCH = 16
DFF = 5632
NFC = 44
EPS = 1e-6
SCALE = 128 ** -0.5


class _Op:
    __slots__ = ("eng", "fn", "deps", "is_dma", "sem", "val", "signal")

    def __init__(self, eng, fn, is_dma, sem):
        self.eng = eng
        self.fn = fn
        self.deps = []
        self.is_dma = is_dma
        self.sem = sem
        self.val = 0
        self.signal = False


class Sched:
    ENGS = ("pe", "act", "dve", "pool", "sp")

    def __init__(self, nc):
        self.nc = nc
        self.eng = {"pe": nc.tensor, "act": nc.scalar, "dve": nc.vector,
                    "pool": nc.gpsimd, "sp": nc.sync}
        self.ops = []
        self.last_write = {}
        self.readers = {}
        self.dma_counts = {}
        self.last_by_sem = {}
        self.barrier_op = None
        self.seen_barrier = set()

    def _add(self, op, reads, writes):
        for k in reads:
            if isinstance(k, tuple) and k[0] == "ps" and k not in writes and op.eng != "pe":
                writes = writes + [k]
        deps = {}
        for k in reads:
            w = self.last_write.get(k)
            if w is not None:
                deps[id(w)] = w
        for k in writes:
            w = self.last_write.get(k)
            if w is not None:
                deps[id(w)] = w
            for r in self.readers.get(k, {}).values():
                deps[id(r)] = r
        if self.barrier_op is not None and op.eng not in self.seen_barrier:
            self.seen_barrier.add(op.eng)
            deps[id(self.barrier_op)] = self.barrier_op
        for d in deps.values():
            if d is op:
                continue
            if d.eng == op.eng and not d.is_dma and not op.is_dma and op.eng == "pe":
                continue
            op.deps.append(d)
            d.signal = True
        for k in reads:
            self.readers.setdefault(k, {})[op.sem] = op
        for k in writes:
            self.last_write[k] = op
            self.readers[k] = {}
        self.last_by_sem[op.sem] = op
        self.ops.append(op)
        return op

    def op(self, eng, fn, reads=(), writes=()):
        return self._add(_Op(eng, fn, False, eng), list(reads), list(writes))

    def dma(self, queue, fn, semkey, reads=(), writes=()):
        o = _Op(queue, fn, True, ("dma", semkey))
        n = self.dma_counts.get(semkey, 0) + 1
        self.dma_counts[semkey] = n
        o.val = 16 * n
        return self._add(o, list(reads), list(writes))

    def barrier(self, fn):
        o = _Op("dve", fn, False, "dve")
        for d in self.last_by_sem.values():
            o.deps.append(d)
            d.signal = True
        self.last_by_sem = {}
        self.last_by_sem[o.sem] = o
        self.ops.append(o)
        self.barrier_op = o
        self.seen_barrier = {"dve"}
        o.signal = True
        return o

    def emit(self, final_wait_eng="sp"):
        nc = self.nc
        cnt = {e: 0 for e in self.ENGS}
        for o in self.ops:
            if not o.is_dma and o.signal:
                cnt[o.eng] += 1
                o.val = cnt[o.eng]
        sems = {}

        def getsem(name):
            if name not in sems:
                sems[name] = nc.alloc_semaphore(name="s%d" % len(sems))
            return sems[name]

        waited = {}
        nwait = 0
        for o in self.ops:
            e = self.eng[o.eng]
            need = {}
            for d in o.deps:
                if d.val > need.get(d.sem, 0):
                    need[d.sem] = d.val
            for s, v in need.items():
                if waited.get((o.eng, s), 0) < v:
                    e.wait_ge(getsem(s), v)
                    waited[(o.eng, s)] = v
                    nwait += 1
            inst = o.fn()
            if o.is_dma:
                inst.then_inc(getsem(o.sem), 16)
            elif o.signal:
                inst.then_inc(getsem(o.sem), 1)
        e = self.eng[final_wait_eng]
        for k, n in self.dma_counts.items():
            e.wait_ge(getsem(("dma", k)), 16 * n)
        return dict(n_ops=len(self.ops), n_wait=nwait, n_sems=len(sems), counts=cnt)


def _blk(W):
    K, N = W.shape
    return np.ascontiguousarray(
        W.reshape(K // 128, 128, N // 128, 128).transpose(2, 1, 0, 3).reshape(N // 128, 128, K))


def _pc(v):
    return np.ascontiguousarray(v.reshape(-1, 128).T)


C_G = 0
C_SCW = 80
C_FCW0 = 104
C_FCW1 = 368
C_FOXB = 632
C_ONES = 760
C_TRI = 888
C_SEL = 1016
NCST = 1144
M_MSB = 0
M_BSG = 2048
M_SGG = 3072
M_WST = 4096
NCM = 5120


class WStream:
    def __init__(self, B, items, la=2):
        self.B, self.items, self.la, self.issued, self.tiles = B, items, la, 0, {}

    def get(self, i):
        while self.issued <= min(i + self.la, len(self.items) - 1):
            src, ncols, eng = self.items[self.issued]
            self.tiles[self.issued] = self.B.load_w(src, ncols, eng)
            self.issued += 1
        return self.tiles.pop(i)


class Builder:
    def __init__(self, debug=False):
        self.debug = debug
        nc = self.nc = bass.Bass("TRN2", target_bir_lowering=False)
        self.S = Sched(nc)
        dt = nc.dram_tensor
        self.xT = dt("xT", [D, T], F32, kind="ExternalInput").ap()
        self.cst_d = dt("cst", [128, NCST], F32, kind="ExternalInput").ap()
        self.cm_d = dt("cm", [128, NCM], F32, kind="ExternalInput").ap()
        self.cmat_d = dt("cmat", [128, 512], F32, kind="ExternalInput").ap()
        self.win = [dt("l0_win", [48, 128, D], F32, kind="ExternalInput").ap(),
                    dt("l1_win", [40, 128, D], F32, kind="ExternalInput").ap()]
        self.wf_d = dt("l1_wf", [128, 128], F32, kind="ExternalInput").ap()
        self.wout = [dt("l%d_wout" % l, [16, 128, D], F32, kind="ExternalInput").ap() for l in range(2)]
        self.wup = [dt("l%d_up" % l, [88, 128, D], F32, kind="ExternalInput").ap() for l in range(2)]
        self.wdn = [dt("l%d_dn" % l, [16, 128, DFF], F32, kind="ExternalInput").ap() for l in range(2)]
        self.y = dt("y", [D, T], F32, kind="ExternalOutput").ap()
        self.xres = dt("xres", [D, T], F32, kind="ExternalOutput" if debug else "Internal").ap()
        if debug:
            self.dbg_h = dt("dbg_h", [128, NCH * T], BF16, kind="ExternalOutput").ap()
            self.dbg_c = dt("dbg_c", [128, 45056], BF16, kind="ExternalOutput").ap()

        a = nc.alloc_sbuf_tensor
        self.hT_raw = a("hT", [128, NCH * T], BF16)
        self.hT = self.hT_raw[:, :].rearrange("p (c t) -> p c t", c=NCH)
        self.hTf = self.hT_raw[:, 0:NCH * 1024].rearrange("p (c t) -> p c t", c=NCH)
        self.hv = self.hT
        self.big = a("big", [128, 45056], BF16)
        self.cat = self.big[:, 0:NCH * T].rearrange("p (c t) -> p c t", c=NCH)
        self.wst = [a("wst%d" % i, [128, 2048], F32) for i in range(2)]
        self.wbf = [a("wbf%d" % i, [128, 2048], BF16) for i in range(3)]
        self.cst = a("cstt", [128, NCST], F32)
        self.cbf = a("cbf", [128, 6 * 128], BF16)
        self.xadd_raw = a("xaddr", [128, 2048], F32)
        self.xadd = [self.xadd_raw[:, i * 512:(i + 1) * 512] for i in range(4)]
        self.scr = a("scr", [128, 6656], BF16)
        self.ps = nc.alloc_psum_tensor("ps", [128, 4096], F32)
        self.wcount = 0
        self.xcount = 0
        self.dummy = a("dmy", [128, 8], F32)

    def bank(self, b, n=512, off=0):
        return self.ps[:, b * 512 + off: b * 512 + off + n]

    def cc(self, col, n=1):
        return self.cst[:, col:col + n]

    def barrier(self):
        nc = self.nc
        d = self.dummy
        self.S.barrier(lambda: nc.vector.memset(d[:, 0:2], 0.0))

    def load_w(self, src, ncols, cast_eng="pool"):
        nc, S = self.nc, self.S
        i = self.wcount
        self.wcount += 1
        st, bf = i % 2, i % 3
        stt, bft = self.wst[st], self.wbf[bf]
        S.dma("sp", lambda: nc.sync.dma_start(out=stt[:, 0:ncols], in_=src), ("ws", st),
              writes=[("ws", st)])
        eng = {"pool": nc.gpsimd, "act": nc.scalar, "dve": nc.vector}[cast_eng]
        if cast_eng == "act":
            fn = lambda: nc.scalar.copy(out=bft[:, 0:ncols], in_=stt[:, 0:ncols])
        else:
            fn = lambda: eng.tensor_copy(out=bft[:, 0:ncols], in_=stt[:, 0:ncols])
        S.op(cast_eng, fn, reads=[("ws", st)], writes=[("wb", bf)])
        return bft, ("wb", bf)

    def mm(self, out, lhsT, rhs, start, stop, reads, writes, skip=False):
        nc = self.nc
        if skip:
            fn = lambda: nc.tensor.matmul(out, lhsT=lhsT, rhs=rhs, start=start, stop=stop,
                                          skip_group_check=True)
        else:
            fn = lambda: nc.tensor.matmul(out, lhsT=lhsT, rhs=rhs, start=start, stop=stop)
        self.S.op("pe", fn, reads=reads, writes=writes)

    def act(self, out, in_, func, reads, writes, scale=1.0, bias=None, accum=None):
        nc = self.nc
        kw = {}
        if bias is not None:
            kw["bias"] = bias
        if accum is not None:
            kw["accum_out"] = accum
        self.S.op("act", lambda: nc.scalar.activation(out=out, in_=in_, func=func, scale=scale, **kw),
                  reads=reads, writes=writes)

    def tt(self, eng, out, in0, in1, op, reads, writes):
        e = {"dve": self.nc.vector, "pool": self.nc.gpsimd}[eng]
        self.S.op(eng, lambda: e.tensor_tensor(out=out, in0=in0, in1=in1, op=op), reads=reads, writes=writes)

    def ts(self, eng, out, in0, s1, op0, reads, writes, s2=None, op1=None):
        e = {"dve": self.nc.vector, "pool": self.nc.gpsimd}[eng]
        if op1 is None:
            fn = lambda: e.tensor_scalar(out=out, in0=in0, scalar1=s1, scalar2=None, op0=op0)
        else:
            fn = lambda: e.tensor_scalar(out=out, in0=in0, scalar1=s1, scalar2=s2, op0=op0, op1=op1)
        self.S.op(eng, fn, reads=reads, writes=writes)

    def stt(self, out, in0, scalar, in1, op0, op1, reads, writes):
        nc = self.nc
        self.S.op("dve", lambda: nc.vector.scalar_tensor_tensor(out=out, in0=in0, scalar=scalar, in1=in1,
                                                                op0=op0, op1=op1), reads=reads, writes=writes)

    def copy(self, eng, out, in_, reads, writes):
        nc = self.nc
        if eng == "act":
            fn = lambda: nc.scalar.copy(out=out, in_=in_)
        else:
            e = {"dve": nc.vector, "pool": nc.gpsimd}[eng]
            fn = lambda: e.tensor_copy(out=out, in_=in_)
        self.S.op(eng, fn, reads=reads, writes=writes)

    def dma(self, out, in_, semkey, reads, writes, queue="sp"):
        nc = self.nc
        e = nc.sync if queue == "sp" else nc.scalar
        self.S.dma(queue, lambda: e.dma_start(out=out, in_=in_), semkey, reads=reads, writes=writes)

    def init(self):
        nc, S = self.nc, self.S
        self.dma(self.cst[:, :], self.cst_d, "cst", [], ["cst"])
        for k, col in enumerate((C_ONES, C_TRI)):
            self.copy("dve", self.cbf[:, k * 128:(k + 1) * 128], self.cst[:, col:col + 128], ["cst"], ["cbf"])
        stg = self.hT_raw[:, 0:1024].bitcast(F32)
        self.dma(stg, self.cmat_d, "cmat", [], ["cmat"])
        self.copy("dve", self.cbf[:, 256:768], stg, ["cmat"], ["cbf"])
        self.ones_bf = self.cbf[:, 0:128]
        self.tri_bf = self.cbf[:, 128:256]
        self.uinc_bf = self.cbf[:, 256:384]
        self.lstr_bf = self.cbf[:, 384:512]
        self.ident_bf = self.cbf[:, 512:640]
        self.negtri_bf = self.cbf[:, 640:768]

    def norm(self, src, gcol, tgs, hcol0, final=False):
        S = self.S
        srcv = src.rearrange("(c p) t -> p c t", p=128)
        slabs = [self.big[:, k * 16384:(k + 1) * 16384].bitcast(F32).rearrange("p (c t) -> p c t", c=NCH)
                 for k in range(2)]
        sq = [self.scr[:, k * 512:(k + 1) * 512] for k in range(2)]
        std = [self.scr[:, 1024 + k * 1024: 2048 + k * 1024].bitcast(F32) for k in range(2)]
        yv = self.y.rearrange("(c p) t -> p c t", p=128)
        for k, tg in enumerate(tgs):
            sl = slabs[k % 2]
            skq = [("slab", k % 2, q) for q in range(4)]
            for q in range(4):
                self.dma(sl[:, q * 4:(q + 1) * 4, :], srcv[:, q * 4:(q + 1) * 4, tg * 512:(tg + 1) * 512],
                         ("slab", k % 2, q), [("xres", c, tg) for c in range(q * 4, q * 4 + 4)], [skq[q]])
            b = k % 2
            for c in range(NCH):
                self.act(sq[c % 2], sl[:, c, :], AF.Square, [skq[c // 4]], [("sq", c % 2)])
                self.mm(self.bank(b), self.ones_bf, sq[c % 2], c == 0, c == NCH - 1,
                        [("sq", c % 2), "cbf"], [("ps", b)])
            st = std[k % 2]
            self.act(st, self.bank(b), AF.Sqrt, [("ps", b)], [("std", k % 2)], scale=1.0 / D, bias=self.eps_ap)
            self.S.op("dve", self._recip(st, st), reads=[("std", k % 2)], writes=[("std", k % 2)])
            for c in range(NCH):
                if final:
                    self.stt(sl[:, c, :], sl[:, c, :], self.cc(gcol + c), st, ALU.mult, ALU.mult,
                             [skq[c // 4], ("std", k % 2), "cst"], [skq[c // 4]])
                else:
                    lt = hcol0 // 512 + k
                    self.stt(self.hv[:, c, lt * 512:(lt + 1) * 512], sl[:, c, :], self.cc(gcol + c), st,
                             ALU.mult, ALU.mult, [skq[c // 4], ("std", k % 2), "cst"], [("hT", c, lt)])
            if final:
                for q in range(4):
                    self.dma(yv[:, q * 4:(q + 1) * 4, tg * 512:(tg + 1) * 512], sl[:, q * 4:(q + 1) * 4, :],
                             ("yout", k % 2, q), [skq[q]], [("y", q, tg)])

    def _recip(self, out, in_):
        nc = self.nc
        return lambda: nc.vector.reciprocal(out=out, in_=in_)

    def _sqsum(self, junk, x, acc):
        nc = self.nc
        return lambda: nc.vector.scalar_tensor_tensor(out=junk, in0=x, scalar=1.0, in1=x, op0=ALU.mult,
                                                      op1=ALU.mult, accum_out=acc)

    def _reduce(self, out, in_, axis):
        nc = self.nc
        return lambda: nc.vector.tensor_reduce(out=out, in_=in_, op=ALU.add, axis=axis)

    def xload(self, src, n, tg):
        j = self.xcount % 4
        self.xcount += 1
        self.dma(self.xadd[j], src[n * 128:(n + 1) * 128, tg * 512:(tg + 1) * 512], ("xl", j),
                 [("xres", n, tg)], [("xadd", j)], queue="act")
        return j

    def xadd_store(self, j, b, n, tg):
        xt = self.xadd[j]
        self.tt("dve", xt, self.bank(b), xt, ALU.add, [("ps", b), ("xadd", j)], [("xadd", j)])
        self.dma(self.xres[n * 128:(n + 1) * 128, tg * 512:(tg + 1) * 512], xt, ("xs", j),
                 [("xadd", j)], [("xres", n, tg)], queue="act")

    def out_proj(self, wd, src):
        tiles = [(n, tg) for n in range(NCH) for tg in range(4)]
        pend = {}
        for idx in range(2):
            pend[idx] = self.xload(src, *tiles[idx])
        w = None
        ws = WStream(self, [(wd[n], D, "dve") for n in range(NCH)])
        for idx, (n, tg) in enumerate(tiles):
            if tg == 0:
                w, wk = ws.get(n)
            if idx + 2 < len(tiles):
                pend[idx + 2] = self.xload(src, *tiles[idx + 2])
            b = idx % 4
            for c in range(NCH):
                self.mm(self.bank(b), w[:, c * 128:(c + 1) * 128], self.cat[:, c, tg * 512:(tg + 1) * 512],
                        c == 0, c == NCH - 1, [wk, ("cat", c, tg)], [("ps", b)])
            self.xadd_store(pend.pop(idx), b, n, tg)

    def proj_fm(self, w, wk, b, lt):
        for c in range(NCH):
            self.mm(self.bank(b), w[:, c * 128:(c + 1) * 128], self.hv[:, c, lt * 512:(lt + 1) * 512],
                    c == 0, c == NCH - 1, [wk, ("hT", c, lt)], [("ps", b)])

    def proj_tm(self, w, wk, b, slot, tt_, ncols=128, wstride=128):
        for c in range(NCH):
            self.mm(self.bank(b, ncols, slot * ncols), self.hT[:, c, tt_ * 128:(tt_ + 1) * 128],
                    w[:, c * wstride:c * wstride + ncols], c == 0, c == NCH - 1,
                    [wk, ("hT", c, tt_ // 4)], [("ps", b)])

    def qkv_bufs(self, st, vt1):
        if st == 0:
            qT = self.scr[:, 0:2048]
            kT = self.scr[:, 2048:4096]
            vt = self.scr[:, 4096:6144].rearrange("p (t d) -> p t d", t=16)
        else:
            xb = self.xadd_raw[:, :].bitcast(BF16)
            qT = xb[:, 0:2048]
            kT = xb[:, 2048:4096]
            vt = vt1.rearrange("p (t d) -> p t d", t=16)
        return qT, kT, vt

    def qkv_gen(self, wd, bq, bk, bv, bufs, st, pb=(0, 1)):
        qT, kT, vt = bufs
        for which, blk_ in (("q", bq), ("k", bk)):
            w, wk = self.load_w(wd[blk_], D)
            dst = qT if which == "q" else kT
            for tg in range(4):
                b = pb[tg % 2]
                for c in range(NCH):
                    self.mm(self.bank(b), w[:, c * 128:(c + 1) * 128], self.hv[:, c, tg * 512:(tg + 1) * 512],
                            c == 0, c == NCH - 1, [wk, ("hT", c, tg)], [("ps", b)])
                    if c % 4 == 3 and c != NCH - 1:
                        yield
                if which == "q":
                    self.act(dst[:, tg * 512:(tg + 1) * 512], self.bank(b), AF.Copy, [("ps", b)],
                             [("q", st, tg)], scale=SCALE)
                else:
                    self.copy("dve", dst[:, tg * 512:(tg + 1) * 512], self.bank(b), [("ps", b)], [("k", st, tg)])
                yield
        w, wk = self.load_w(wd[bv], D)
        for t4 in range(4):
            b = pb[t4 % 2]
            for s_ in range(4):
                self.proj_tm(w, wk, b, s_, t4 * 4 + s_)
                if s_ != 3:
                    yield
            self.copy("dve", vt[:, t4 * 4:(t4 + 1) * 4, :],
                      self.bank(b).rearrange("p (t d) -> p t d", t=4), [("ps", b)], [("v", st, t4)])
            yield

    def mixer0(self):
        S = self.S
        wd = self.win[0]
        big = self.big
        sp0 = 32768
        u_full = big[:, sp0:sp0 + 4104].bitcast(F32)
        gcsb = big[:, sp0 + 4104:sp0 + 5128].bitcast(F32)
        acc = big[:, sp0 + 5128:sp0 + 6152].bitcast(F32)
        nc = self.nc
        S.op("pool", lambda: nc.gpsimd.memset(u_full[:, 0:2], 0.0), writes=[("u", -1)])
        import os
        parts = os.environ.get("MIX0", "AB")
        for g in range(8 if "B" in parts else 0):
            wgb, kgb = self.load_w(wd[24 + g], D)
            wgc, kgc = self.load_w(wd[32 + g], D)
            whi, khi = self.load_w(wd[40 + g], D)
            for tg in range(4):
                b0 = (tg % 2) * 3
                self.proj_fm(wgb, kgb, b0, tg)
                self.proj_fm(wgc, kgc, b0 + 1, tg)
                self.proj_fm(whi, khi, b0 + 2, tg)
                self.copy("act", gcsb, self.bank(b0 + 1), [("ps", b0 + 1)], ["gcsb"])
                c0 = 2 + tg * 512
                self.tt("dve", u_full[:, c0:c0 + 512], self.bank(b0 + 2), gcsb, ALU.mult,
                        [("ps", b0 + 2), "gcsb"], [("u", tg)])
                wc = C_SCW + g * 3
                self.ts("dve", acc, u_full[:, c0:c0 + 512], self.cc(wc + 2), ALU.mult, [("u", tg), "cst"], ["acc"])
                self.stt(acc, u_full[:, c0 - 1:c0 + 511], self.cc(wc + 1), acc, ALU.mult, ALU.add,
                         [("u", tg), ("u", tg - 1), "acc", "cst"], ["acc"])
                self.stt(acc, u_full[:, c0 - 2:c0 + 510], self.cc(wc), acc, ALU.mult, ALU.add,
                         [("u", tg), ("u", tg - 1), "acc", "cst"], ["acc"])
                self.tt("dve", self.cat[:, 8 + g, tg * 512:(tg + 1) * 512], self.bank(b0), acc, ALU.mult,
                        [("ps", b0), "acc"], [("cat", 8 + g, tg)])
        self.barrier()
        msk = big[:, sp0:sp0 + 2048]
        mst = big[:, sp0 + 2048:sp0 + 6144].bitcast(F32)
        self.dma(mst, self.cm_d[:, M_MSB:M_MSB + 2048], "cmm", [], ["mst"])
        self.copy("dve", msk, mst, ["mst"], ["msk"])
        self.barrier()
        o = sp0 + 2048
        zsb = [[big[:, o + (2 * sl + k) * 1024:o + (2 * sl + k + 1) * 1024].bitcast(F32) for k in range(2)]
               for sl in range(2)]
        o += 4096
        esb = [big[:, o + k * 1024:o + (k + 1) * 1024].bitcast(F32) for k in range(1)]
        o += 1024
        spb = [[big[:, o + (2 * sl + k) * 512:o + (2 * sl + k + 1) * 512] for k in range(2)] for sl in range(2)]
        o += 2048
        abf = [[big[:, o + sl * 512:o + (sl + 1) * 512]] * 2 for sl in range(2)]
        o += 1024
        vt1 = big[:, o:o + 2048]
        o += 2048
        assert o <= 45056
        bufsets = [self.qkv_bufs(0, None), self.qkv_bufs(1, vt1)]
        self.zcnt = 0
        self.ecnt = 0

        def sb_stream(h, G, sl, st, accb, ob):
            qT, kT, vt = bufsets[st]
            blocks = list(range(4 * G + 3, -1, -1))
            qs = qT[:, G * 512:(G + 1) * 512]
            nb = len(blocks)
            zbank = {}

            def front(i, cn):
                zb = 2 + self.zcnt % 2
                self.zcnt += 1
                e = esb[0]
                r = i - 4 * G
                self.mm(self.bank(zb), kT[:, i * 128:(i + 1) * 128], qs, True, r < 0,
                        [("k", st, i // 4), ("q", st, G)], [("ps", zb)])
                if r >= 0:
                    self.mm(self.bank(zb), self.ident_bf, msk[:, r * 512:(r + 1) * 512], False, True,
                            ["cbf", "msk"], [("ps", zb)])
                self.act(e, self.bank(zb), AF.Exp, [("ps", zb)], ["esb"])
                self.act(spb[sl][cn % 2], e, AF.Ln, ["esb"], [("sp", sl, cn % 2)], bias=self.one_ap)
                zbank[cn % 2] = zb

            def zc(cn):
                zb = zbank[cn % 2]
                self.copy("dve", zsb[sl][cn % 2], self.bank(zb), [("ps", zb)], [("zsb", sl, cn % 2)])

            def back_u(i, cn, first, last):
                self.mm(self.bank(accb), self.uinc_bf, spb[sl][cn % 2], first, False,
                        [("sp", sl, cn % 2), "cbf"], [("ps", accb)], skip=True)

            def back_b(i, cn, first, last):
                self.tt("dve", zsb[sl][cn % 2], zsb[sl][cn % 2], self.bank(accb), ALU.subtract,
                        [("zsb", sl, cn % 2), ("ps", accb)], [("zsb", sl, cn % 2)])
                if not last:
                    self.mm(self.bank(accb), self.lstr_bf, spb[sl][cn % 2], False, False,
                            [("sp", sl, cn % 2), "cbf"], [("ps", accb)], skip=True)
                self.act(abf[sl][cn % 2], zsb[sl][cn % 2], AF.Exp, [("zsb", sl, cn % 2)], [("abf", sl)])

            def back_av(i, cn, first, last):
                self.mm(self.bank(ob), vt[:, i, :], abf[sl][cn % 2], first, last,
                        [("v", st, i // 4), ("abf", sl)], [("ps", ob)])

            front(blocks[0], 0)
            yield
            zc(0)
            yield
            for k in range(nb):
                if k + 1 < nb:
                    front(blocks[k + 1], k + 1)
                    yield
                args = (blocks[k], k, k == 0, k == nb - 1)
                back_u(*args)
                yield
                back_b(*args)
                yield
                if k + 1 < nb:
                    zc(k + 1)
                back_av(*args)
                yield
            self.copy("dve", self.cat[:, h, G * 512:(G + 1) * 512], self.bank(ob), [("ps", ob)],
                      [("cat", h, G)])

        nh = int(os.environ.get("NH", "8")) if "A" in parts else 0
        if nh:
            for _ in self.qkv_gen(wd, 0, 8, 16, bufsets[0], 0):
                pass
        for h in range(nh):
            st = h % 2
            pg = self.qkv_gen(wd, h + 1, 9 + h, 17 + h, bufsets[1 - st], 1 - st) if h + 1 < nh else iter(())
            todo = [3, 2, 1, 0]
            free_slots = [(0, 4, 6), (1, 5, 7)]
            active = []
            steps = 0
            while todo or active:
                while todo and free_slots:
                    sl, accb, ob = free_slots.pop(0)
                    active.append((sb_stream(h, todo.pop(0), sl, st, accb, ob), (sl, accb, ob)))
                for item in list(active):
                    g, slot = item
                    try:
                        next(g)
                    except StopIteration:
                        active.remove(item)
                        free_slots.append(slot)
                    steps += 1
                    if steps % 3 == 0:
                        next(pg, None)
            for _ in pg:
                pass
        self.barrier()

    def ffn(self, layer, gcol):
        S = self.S
        wup, wdn = self.wup[layer], self.wdn[layer]
        fcw = C_FCW0 if layer == 0 else C_FCW1
        gT = self.big[:, 0:NFC * 1024].rearrange("p (f t) -> p f t", f=NFC)
        h2 = self.hT_raw[:, 16384:32768]
        cbuf = {}
        for k in range(2):
            cbuf[("g", k)] = h2[:, k * 4096:k * 4096 + 2048].bitcast(F32)
            cbuf[("u", k)] = h2[:, k * 4096 + 2048:k * 4096 + 4096].bitcast(F32)
        halo = h2[:, 8192:8192 + 352].bitcast(F32)
        self.hv = self.hTf
        for half in range(2):
            self.norm(self.xres, gcol, [half * 2, half * 2 + 1], 0)
            self.barrier()
            ws = WStream(self, [(wup[(jj // 2) + NFC * (jj % 2)], D, "act") for jj in range(2 * NFC)])
            for j in range(NFC):
                k = j % 2
                for which, blk_, b0 in (("g", j, 4 * k), ("u", NFC + j, 4 * k + 2)):
                    w, wk = ws.get(2 * j + (0 if which == "g" else 1))
                    for lt in range(2):
                        self.proj_fm(w, wk, b0 + lt, lt)
                    P = self.ps[:, b0 * 512:b0 * 512 + 1024]
                    pk = [("ps", b0), ("ps", b0 + 1)]
                    cb = cbuf[(which, k)]
                    ck = ("cb", which, k)
                    wc = fcw + blk_ * 3
                    self.act(cb, P, AF.Identity, pk, [ck], scale=self.cc(wc + 2))
                    self.stt(cb[:, 1:1024], P[:, 0:1023], self.cc(wc + 1), cb[:, 1:1024], ALU.mult, ALU.add,
                             pk + [ck, "cst"], [ck])
                    self.stt(cb[:, 2:1024], P[:, 0:1022], self.cc(wc), cb[:, 2:1024], ALU.mult, ALU.add,
                             pk + [ck, "cst"], [ck])
                    hl = halo[:, blk_ * 2:blk_ * 2 + 2]
                    if half == 0:
                        self.copy("act", hl, P[:, 1022:1024], pk, [("halo", blk_)])
                    else:
                        hk = ("halo", blk_)
                        self.stt(cb[:, 0:1], hl[:, 1:2], self.cc(wc + 1), cb[:, 0:1], ALU.mult, ALU.add,
                                 [hk, ck, "cst"], [ck])
                        self.stt(cb[:, 0:1], hl[:, 0:1], self.cc(wc), cb[:, 0:1], ALU.mult, ALU.add,
                                 [hk, ck, "cst"], [ck])
                        self.stt(cb[:, 1:2], hl[:, 1:2], self.cc(wc), cb[:, 1:2], ALU.mult, ALU.add,
                                 [hk, ck, "cst"], [ck])
                cg, cu = cbuf[("g", k)], cbuf[("u", k)]
                self.act(cg, cg, AF.Silu, [("cb", "g", k)], [("cb", "g", k)])
                self.tt("pool", gT[:, j, :], cg, cu, ALU.mult, [("cb", "g", k), ("cb", "u", k)], [("gT", j)])
            tiles = [(n, lt) for n in range(NCH) for lt in range(2)]
            pend = {}
            for idx in range(2):
                n, lt = tiles[idx]
                pend[idx] = self.xload(self.xres, n, half * 2 + lt)
            ws = WStream(self, [(wdn[nn // 4][:, (nn % 4) * 1408:(nn % 4 + 1) * 1408], 1408, "dve")
                                for nn in range(4 * NCH)])
            for n in range(NCH):
                b0 = (n % 4) * 2
                for piece in range(4):
                    w, wk = ws.get(n * 4 + piece)
                    for lt in range(2):
                        for fl in range(11):
                            fc = piece * 11 + fl
                            self.mm(self.bank(b0 + lt), w[:, fl * 128:(fl + 1) * 128],
                                    gT[:, fc, lt * 512:(lt + 1) * 512], fc == 0, fc == NFC - 1,
                                    [wk, ("gT", fc)], [("ps", b0 + lt)])
                for lt in range(2):
                    idx = n * 2 + lt
                    if idx + 2 < len(tiles):
                        n2, lt2 = tiles[idx + 2]
                        pend[idx + 2] = self.xload(self.xres, n2, half * 2 + lt2)
                    self.xadd_store(pend.pop(idx), b0 + lt, n, half * 2 + lt)
            self.barrier()
        self.hv = self.hT

    def mixer1(self):
        S, nc = self.S, self.nc
        wd = self.win[1]
        big = self.big
        sp0 = 32768
        bsg = big[:, sp0:sp0 + 2048].bitcast(F32)
        sgg = big[:, sp0 + 2048:sp0 + 3072]
        wst = big[:, sp0 + 3072:sp0 + 4096]
        o = sp0 + 4096
        stg = big[:, o:o + 2048].bitcast(F32)
        o += 2048
        ug = big[:, o:o + 2048]
        o += 2048
        tmp = [big[:, o + k * 1024:o + (k + 1) * 1024].bitcast(F32) for k in range(2)]
        o += 2048
        small = big[:, o:o + 2048].bitcast(F32)
        gv = big[:, 8 * T:16 * T].rearrange("p (t c) -> p t c", t=16)
        GV = "gv"
        self.dma(bsg, self.cm_d[:, M_BSG:M_BSG + 1024], "cm0", [], ["bsg"])
        self.dma(stg, self.cm_d[:, M_SGG:M_SGG + 1024], "cm1", [], ["stg"])
        self.copy("dve", sgg, stg, ["stg"], ["sgg"])
        self.dma(stg, self.cm_d[:, M_WST:M_WST + 1024], "cm2", [], ["stg"])
        for g in range(8):
            self.tt("dve", wst[:, g * 128:(g + 1) * 128], stg[:, g * 128:(g + 1) * 128],
                    self.cc(C_TRI, 128), ALU.mult, ["stg", "cst"], ["wst"])
        AX = mybir.AxisListType.X
        s1p = small[:, 128:256]
        s2p = small[:, 256:384]
        junk = small[:, 384:448].bitcast(BF16)
        s1 = small[:, 0:16]
        s2 = small[:, 16:32]
        mean = small[:, 32:48]
        msq = small[:, 48:64]
        var = small[:, 64:80]
        for bb in range(8):
            w, wk = self.load_w(wd[8 + bb], D)
            for t4 in range(4):
                b = 2 + t4 % 2
                for s in range(4):
                    self.proj_tm(w, wk, b, s, t4 * 4 + s)
                for s in range(4):
                    tt_ = t4 * 4 + s
                    gsl = gv[:, tt_, bb * 128:(bb + 1) * 128]
                    col = tt_ * 8 + bb
                    self.act(gsl, self.bank(b, 128, s * 128), AF.Gelu, [("ps", b)], [("gvp", tt_ % 4)],
                             accum=s1p[:, col:col + 1])
                    self.S.op("dve", self._sqsum(junk, gsl, s2p[:, col:col + 1]), reads=[("gvp", tt_ % 4)],
                              writes=["junk", "s2p"])
        self.S.op("dve", self._reduce(s1, s1p.rearrange("p (t b) -> p t b", b=8), AX),
                  reads=[("gvp", k) for k in range(4)] + ["s2p"], writes=["s1"])
        self.S.op("dve", self._reduce(s2, s2p.rearrange("p (t b) -> p t b", b=8), AX), reads=["s2p"], writes=["s2"])
        self.ts("dve", mean, s1, 1.0 / 1024, ALU.mult, ["s1"], ["mean"])
        self.tt("dve", msq, mean, mean, ALU.mult, ["mean"], ["msq"])
        self.stt(var, s2, 1.0 / 1024, msq, ALU.mult, ALU.subtract, ["s2", "msq"], ["var"])
        self.act(var, var, AF.Sqrt, ["var"], ["var"], bias=self.eps_ap)
        self.S.op("dve", self._recip(var, var), reads=["var"], writes=["var"])
        ugs = [ug, stg.bitcast(BF16)[:, 0:2048]]

        def uproj(g):
            w, wk = self.load_w(wd[g], D)
            for tg in range(4):
                b = tg % 2
                self.proj_fm(w, wk, b, tg)
                self.act(ugs[g % 2][:, tg * 512:(tg + 1) * 512], self.bank(b), AF.Gelu, [("ps", b)],
                         [("ug", g % 2, tg)] + (["stg"] if g == 1 else []))

        def mixing(g):
            for tg in range(4):
                b = 4 + tg % 2
                for s in range(4):
                    n = tg * 4 + s
                    self.mm(self.bank(b, 128, s * 128), gv[:, n, g * 128:(g + 1) * 128],
                            wst[:, g * 128:(g + 1) * 128], True, True, [GV, "wst"], [("ps", b)])
                tm = tmp[tg % 2]
                self.tt("dve", tm.rearrange("p (s t) -> p s t", s=4),
                        self.bank(b).rearrange("p (s t) -> p s t", s=4),
                        bsg[:, g * 128:(g + 1) * 128].unsqueeze(1).to_broadcast([128, 4, 128]), ALU.add,
                        [("ps", b), "bsg"], [("tmp", tg % 2)])
                self.tt("dve", self.cat[:, g, tg * 512:(tg + 1) * 512], tm, ugs[g % 2][:, tg * 512:(tg + 1) * 512],
                        ALU.mult, [("tmp", tg % 2), ("ug", g % 2, tg)], [("cat", g, tg)])

        uproj(0)
        uproj(1)
        for tt_ in range(16):
            self.ts("dve", gv[:, tt_, :], gv[:, tt_, :], mean[:, tt_:tt_ + 1], ALU.subtract,
                    [("gvp", k) for k in range(4)] + ["mean", "var"], [GV],
                    s2=var[:, tt_:tt_ + 1], op1=ALU.mult)
            self.tt("dve", gv[:, tt_, :], gv[:, tt_, :], sgg, ALU.mult, [GV, "sgg"], [GV])
        for g in range(8):
            mixing(g)
            if g + 2 < 8:
                uproj(g + 2)
        self.barrier()
        xf = small[:, 128:256]
        lf = small[:, 256:384]
        pfx = small[:, 384:512]
        negc = small[:, 512:640]
        rs = small[:, 640:768]
        bt = small[:, 768:1024]
        wf, wfk = self.load_w(self.wf_d, 128, cast_eng="dve")
        for tt_ in range(16):
            self.proj_tm(wf, wfk, 0, tt_, tt_, ncols=8, wstride=8)
        self.tt("dve", xf, self.bank(0, 128), self.cc(C_FOXB, 128), ALU.add, [("ps", 0), "cst"], ["xf"])
        self.act(xf, xf, AF.Exp, ["xf"], ["xf"], scale=-1.0)
        self.act(lf, xf, AF.Ln, ["xf"], ["lf"], bias=self.one_ap)
        self.mm(self.bank(1, 128), self.cc(C_TRI, 128), lf, True, True, ["lf", "cst"], [("ps", 1)])
        self.mm(self.bank(2, 128), self.cc(C_ONES, 128), lf, True, True, ["lf", "cst"], [("ps", 2)])
        S.op("dve", lambda: nc.vector.memset(pfx[:, 0:8], 0.0), writes=["pfx"])
        for tt_ in range(1, 16):
            self.tt("dve", pfx[:, tt_ * 8:(tt_ + 1) * 8], pfx[:, (tt_ - 1) * 8:tt_ * 8],
                    self.bank(2, 8, (tt_ - 1) * 8), ALU.add, ["pfx", ("ps", 2)], ["pfx"])
        self.tt("dve", negc, pfx, self.bank(1, 128), ALU.add, ["pfx", ("ps", 1)], ["negc"])
        self.mm(self.bank(3, 128), self.cc(C_SEL, 128), negc, True, True, ["negc", "cst"], [("ps", 3)])
        self.copy("dve", rs, self.bank(3, 128), [("ps", 3)], ["rs"])
        self.barrier()
        pbf = [stg.bitcast(BF16)[:, k * 128:(k + 1) * 128] for k in range(8)]
        rden = tmp[0][:, 0:128]
        rsv = rs.rearrange("p (t h) -> p t h", h=8)
        bts = [bt, small[:, 128:384]]
        vt1 = big[:, sp0:sp0 + 2048]
        bufsets = [self.qkv_bufs(0, None), self.qkv_bufs(1, vt1)]
        blocks = [(tb, i) for tb in range(16) for i in range(tb + 1)]
        LA = 4

        def fox_gen(h, st):
            qT, kT, vt = bufsets[st]
            btt = bts[h % 2]
            btk = ("bt", h % 2)
            for i in range(16):
                self.ts("dve", btt[:, i * 16:(i + 1) * 16], rsv[:, :, h], -1.0, ALU.mult, ["rs", "negc"], [btk],
                        s2=negc[:, i * 8 + h:i * 8 + h + 1], op1=ALU.add)
            yield

            def zstep(n):
                tb, i = blocks[n]
                zb = 1 + n % 5
                p, pk = pbf[n % 8], ("pbf", n % 8)
                self.mm(self.bank(zb, 128), kT[:, i * 128:(i + 1) * 128], qT[:, tb * 128:(tb + 1) * 128],
                        True, i != tb, [("k", st, i // 4), ("q", st, tb // 4)], [("ps", zb)])
                if i == tb:
                    self.mm(self.bank(zb, 128), self.ident_bf, self.negtri_bf, False, True, ["cbf"], [("ps", zb)])
                self.act(p, self.bank(zb, 128), AF.Exp, [("ps", zb), btk], [pk],
                         bias=btt[:, i * 16 + tb:i * 16 + tb + 1])

            def ostep(n):
                tb, i = blocks[n]
                ob = 6 + tb % 2
                p, pk = pbf[n % 8], ("pbf", n % 8)
                self.mm(self.bank(ob, 128), vt[:, i, :], p, i == 0, i == tb, [("v", st, i // 4), pk],
                        [("ps", ob)], skip=True)
                self.mm(self.bank(ob, 128, 128), self.ones_bf, p, False, i == tb, [pk, "cbf"], [("ps", ob)],
                        skip=True)
                if i == tb:
                    self.S.op("dve", self._recip(rden, self.bank(ob, 128, 128)), reads=[("ps", ob)],
                              writes=["rden"])
                    self.tt("dve", self.cat[:, 8 + h, tb * 128:(tb + 1) * 128], self.bank(ob, 128), rden,
                            ALU.mult, [("ps", ob), "rden"], [("cat", 8 + h, tb // 4), GV])

            for n in range(len(blocks) + LA):
                if n < len(blocks):
                    zstep(n)
                if n - LA >= 0:
                    ostep(n - LA)
                yield

        for _ in self.qkv_gen(wd, 16, 24, 32, bufsets[0], 0, pb=(0, 0)):
            pass
        for h in range(8):
            st = h % 2
            pg = self.qkv_gen(wd, 17 + h, 25 + h, 33 + h, bufsets[1 - st], 1 - st, pb=(0, 0)) if h + 1 < 8 else iter(())
            steps = 0
            for _ in fox_gen(h, st):
                steps += 1
                if steps % 3 == 0:
                    next(pg, None)
            for _ in pg:
                pass
        self.barrier()

    def build(self, upto=99):
        nc, S = self.nc, self.S
        self.init()
        S.op("dve", lambda: nc.vector.memset(self.dummy[:, 2:3], EPS), writes=["dummy"])
        S.op("dve", lambda: nc.vector.memset(self.dummy[:, 3:4], 1.0), writes=["dummy"])
        self.eps_ap = self.dummy[:, 2:3]
        self.one_ap = self.dummy[:, 3:4]
        self.barrier()
        steps = [
            lambda: self.norm(self.xT, C_G + 0, [0, 1, 2, 3], 0),
            lambda: self.mixer0(),
            lambda: self.out_proj(self.wout[0], self.xT),
            lambda: self.ffn(0, C_G + 16),
            lambda: self.norm(self.xres, C_G + 32, [0, 1, 2, 3], 0),
            lambda: self.mixer1(),
            lambda: self.out_proj(self.wout[1], self.xres),
            lambda: self.ffn(1, C_G + 48),
            lambda: self.norm(self.xres, C_G + 64, [0, 1, 2, 3], 0, final=True),
        ]
        for i, st in enumerate(steps):
            if i > upto:
                break
            st()
            self.barrier()
        if self.debug:
            self.dma(self.dbg_h, self.hT_raw[:, :], "dbgh", [], [])
            self.dma(self.dbg_c, self.big[:, :], "dbgc", [], [])
        info = S.emit()
        return nc, info


_CACHE = {}


def _prep_shared(inp):
    f = lambda k: np.asarray(inp[k], dtype=np.float32)
    cst = np.zeros((128, NCST), np.float32)
    for i, k in enumerate(("l0_mix_norm_g", "l0_ffn_norm_g", "l1_mix_norm_g", "l1_ffn_norm_g", "final_norm_g")):
        cst[:, C_G + 16 * i:C_G + 16 * (i + 1)] = _pc(f(k))
    scw = f("l0_sc_conv_w")
    cst[:, C_SCW:C_SCW + 24] = scw.reshape(3, 8, 128).transpose(2, 1, 0).reshape(128, 24)
    for col, k in ((C_FCW0, "l0_ffn_conv_w"), (C_FCW1, "l1_ffn_conv_w")):
        cw = f(k)
        cst[:, col:col + 264] = cw.reshape(3, 88, 128).transpose(2, 1, 0).reshape(128, 264)
    cst[:, C_FOXB:C_FOXB + 128] = np.tile(f("l1_fox_b_f"), 16)[None, :]
    r = np.arange(128)
    cst[:, C_ONES:C_ONES + 128] = 1.0
    cst[:, C_TRI:C_TRI + 128] = (r[:, None] <= r[None, :])
    cst[:, C_SEL:C_SEL + 128] = (r[:, None] == 64)
    cmat = np.zeros((128, 512), np.float32)
    cmat[:, 0:128] = (r[:, None] >= r[None, :])
    cmat[:, 128:256] = (r[:, None] < r[None, :])
    cmat[:, 256:384] = (r[:, None] == r[None, :])
    cmat[:, 384:512] = np.where(r[:, None] <= r[None, :], 0.0, -30000.0)
    cm = np.zeros((128, NCM), np.float32)
    tcol = np.arange(512)
    for q in range(4):
        cm[:, M_MSB + q * 512:M_MSB + (q + 1) * 512] = np.where((128 * q + r[:, None]) < tcol[None, :], 0.0, -30000.0)
    cm[:, M_BSG:M_BSG + 1024] = f("l1_sg_b").reshape(1, 1024)
    cm[:, M_SGG:M_SGG + 1024] = f("l1_sg_norm_g").reshape(1, 1024)
    cm[:, M_WST:M_WST + 1024] = f("l1_sg_w").transpose(2, 0, 1).reshape(128, 1024)
    w1 = f("l1_w_in")
    sh = {
        "cst": cst, "cm": cm, "cmat": cmat,
        "l0_win": _blk(f("l0_w_in")), "l1_win": _blk(w1[:, :5120]),
        "l1_wf": np.ascontiguousarray(w1[:, 5120:5128].reshape(16, 128, 8).transpose(1, 0, 2).reshape(128, 128)),
        "l0_wout": _blk(f("l0_w_out")), "l1_wout": _blk(f("l1_w_out")),
        "l0_up": _blk(f("l0_ffn_up")), "l1_up": _blk(f("l1_ffn_up")),
        "l0_dn": _blk(f("l0_ffn_down")), "l1_dn": _blk(f("l1_ffn_down")),
    }
    return sh


def kernel(**inputs):
    x = np.asarray(inputs["x"], dtype=np.float32)
    if "nc" not in _CACHE:
        _CACHE["nc"], _CACHE["info"] = Builder().build()
    nc = _CACHE["nc"]
    sh = _prep_shared(inputs)
    in_maps = []
    for b in range(8):
        m = dict(sh)
        m["xT"] = np.ascontiguousarray(x[b].T)
        in_maps.append(m)
    res = run_bass_kernel_spmd(nc, in_maps, core_ids=list(range(8)))
    out = np.stack([np.ascontiguousarray(res.results[b]["y"].T) for b in range(8)], axis=0)
    return out.astype(np.float32)
```

```python
import numpy as np
import concourse.bass as bass
import concourse.mybir as mybir
from concourse.bass_utils import run_bass_kernel_spmd

F32 = mybir.dt.float32
BF16 = mybir.dt.bfloat16
AF = mybir.ActivationFunctionType
ALU = mybir.AluOpType

T = 2048
D = 2048
NCH = 16
DFF = 5632
NFC = 44
EPS = 1e-6
SCALE = 128 ** -0.5


class _Op:
    __slots__ = ("eng", "fn", "deps", "is_dma", "sem", "val", "signal")

    def __init__(self, eng, fn, is_dma, sem):
        self.eng = eng
        self.fn = fn
        self.deps = []
        self.is_dma = is_dma
        self.sem = sem
        self.val = 0
        self.signal = False


class Sched:
    ENGS = ("pe", "act", "dve", "pool", "sp")

    def __init__(self, nc):
        self.nc = nc
        self.eng = {"pe": nc.tensor, "act": nc.scalar, "dve": nc.vector,
                    "pool": nc.gpsimd, "sp": nc.sync}
        self.ops = []
        self.last_write = {}
        self.readers = {}
        self.dma_counts = {}
        self.last_by_sem = {}
        self.barrier_op = None
        self.seen_barrier = set()

    def _add(self, op, reads, writes):
        for k in reads:
            if isinstance(k, tuple) and k[0] == "ps" and k not in writes and op.eng != "pe":
                writes = writes + [k]
        deps = {}
        for k in reads:
            w = self.last_write.get(k)
            if w is not None:
                deps[id(w)] = w
        for k in writes:
            w = self.last_write.get(k)
            if w is not None:
                deps[id(w)] = w
            for r in self.readers.get(k, {}).values():
                deps[id(r)] = r
        if self.barrier_op is not None and op.eng not in self.seen_barrier:
            self.seen_barrier.add(op.eng)
            deps[id(self.barrier_op)] = self.barrier_op
        for d in deps.values():
            if d is op:
                continue
            if d.eng == op.eng and not d.is_dma and not op.is_dma and op.eng == "pe":
                continue
            op.deps.append(d)
            d.signal = True
        for k in reads:
            self.readers.setdefault(k, {})[op.sem] = op
        for k in writes:
            self.last_write[k] = op
            self.readers[k] = {}
        self.last_by_sem[op.sem] = op
        self.ops.append(op)
        return op

    def op(self, eng, fn, reads=(), writes=()):
        return self._add(_Op(eng, fn, False, eng), list(reads), list(writes))

    def dma(self, queue, fn, semkey, reads=(), writes=()):
        o = _Op(queue, fn, True, ("dma", semkey))
        n = self.dma_counts.get(semkey, 0) + 1
        self.dma_counts[semkey] = n
        o.val = 16 * n
        return self._add(o, list(reads), list(writes))

    def barrier(self, fn):
        o = _Op("dve", fn, False, "dve")
        for d in self.last_by_sem.values():
            o.deps.append(d)
            d.signal = True
        self.last_by_sem = {}
        self.last_by_sem[o.sem] = o
        self.ops.append(o)
        self.barrier_op = o
        self.seen_barrier = {"dve"}
        o.signal = True
        return o

    def emit(self, final_wait_eng="sp"):
        nc = self.nc
        cnt = {e: 0 for e in self.ENGS}
        for o in self.ops:
            if not o.is_dma and o.signal:
                cnt[o.eng] += 1
                o.val = cnt[o.eng]
        sems = {}

        def getsem(name):
            if name not in sems:
                sems[name] = nc.alloc_semaphore(name="s%d" % len(sems))
            return sems[name]

        waited = {}
        nwait = 0
        for o in self.ops:
            e = self.eng[o.eng]
            need = {}
            for d in o.deps:
                if d.val > need.get(d.sem, 0):
                    need[d.sem] = d.val
            for s, v in need.items():
                if waited.get((o.eng, s), 0) < v:
                    e.wait_ge(getsem(s), v)
                    waited[(o.eng, s)] = v
                    nwait += 1
            inst = o.fn()
            if o.is_dma:
                inst.then_inc(getsem(o.sem), 16)
            elif o.signal:
                inst.then_inc(getsem(o.sem), 1)
        e = self.eng[final_wait_eng]
        for k, n in self.dma_counts.items():
            e.wait_ge(getsem(("dma", k)), 16 * n)
        return dict(n_ops=len(self.ops), n_wait=nwait, n_sems=len(sems), counts=cnt)


def _blk(W):
    K, N = W.shape
    return np.ascontiguousarray(
        W.reshape(K // 128, 128, N // 128, 128).transpose(2, 1, 0, 3).reshape(N // 128, 128, K))


def _pc(v):
    return np.ascontiguousarray(v.reshape(-1, 128).T)


C_G = 0
C_SCW = 80
C_FCW0 = 104
C_FCW1 = 368
C_FOXB = 632
C_ONES = 760
C_TRI = 888
C_SEL = 1016
NCST = 1144
M_MSB = 0
M_BSG = 2048
M_SGG = 3072
M_WST = 4096
NCM = 5120


class WStream:
    def __init__(self, B, items, la=2):
        self.B, self.items, self.la, self.issued, self.tiles = B, items, la, 0, {}

    def get(self, i):
        while self.issued <= min(i + self.la, len(self.items) - 1):
            src, ncols, eng = self.items[self.issued]
            self.tiles[self.issued] = self.B.load_w(src, ncols, eng)
            self.issued += 1
        return self.tiles.pop(i)


class Builder:
    def __init__(self, debug=False):
        self.debug = debug
        nc = self.nc = bass.Bass("TRN2", target_bir_lowering=False)
        self.S = Sched(nc)
        dt = nc.dram_tensor
        self.xT = dt("xT", [D, T], F32, kind="ExternalInput").ap()
        self.cst_d = dt("cst", [128, NCST], F32, kind="ExternalInput").ap()
        self.cm_d = dt("cm", [128, NCM], F32, kind="ExternalInput").ap()
        self.cmat_d = dt("cmat", [128, 512], F32, kind="ExternalInput").ap()
        self.win = [dt("l0_win", [48, 128, D], F32, kind="ExternalInput").ap(),
                    dt("l1_win", [40, 128, D], F32, kind="ExternalInput").ap()]
        self.wf_d = dt("l1_wf", [128, 128], F32, kind="ExternalInput").ap()
        self.wout = [dt("l%d_wout" % l, [16, 128, D], F32, kind="ExternalInput").ap() for l in range(2)]
        self.wup = [dt("l%d_up" % l, [88, 128, D], F32, kind="ExternalInput").ap() for l in range(2)]
        self.wdn = [dt("l%d_dn" % l, [16, 128, DFF], F32, kind="ExternalInput").ap() for l in range(2)]
        self.y = dt("y", [D, T], F32, kind="ExternalOutput").ap()
        self.xres = dt("xres", [D, T], F32, kind="ExternalOutput" if debug else "Internal").ap()
        if debug:
            self.dbg_h = dt("dbg_h", [128, NCH * T], BF16, kind="ExternalOutput").ap()
            self.dbg_c = dt("dbg_c", [128, 45056], BF16, kind="ExternalOutput").ap()

        a = nc.alloc_sbuf_tensor
        self.hT_raw = a("hT", [128, NCH * T], BF16)
        self.hT = self.hT_raw[:, :].rearrange("p (c t) -> p c t", c=NCH)
        self.hTf = self.hT_raw[:, 0:NCH * 1024].rearrange("p (c t) -> p c t", c=NCH)
        self.hv = self.hT
        self.big = a("big", [128, 45056], BF16)
        self.cat = self.big[:, 0:NCH * T].rearrange("p (c t) -> p c t", c=NCH)
        self.wst = [a("wst%d" % i, [128, 2048], F32) for i in range(2)]
        self.wbf = [a("wbf%d" % i, [128, 2048], BF16) for i in range(3)]
        self.cst = a("cstt", [128, NCST], F32)
        self.cbf = a("cbf", [128, 6 * 128], BF16)
        self.xadd_raw = a("xaddr", [128, 2048], F32)
        self.xadd = [self.xadd_raw[:, i * 512:(i + 1) * 512] for i in range(4)]
        self.scr = a("scr", [128, 6656], BF16)
        self.ps = nc.alloc_psum_tensor("ps", [128, 4096], F32)
        self.wcount = 0
        self.xcount = 0
        self.dummy = a("dmy", [128, 8], F32)

    def bank(self, b, n=512, off=0):
        return self.ps[:, b * 512 + off: b * 512 + off + n]

    def cc(self, col, n=1):
        return self.cst[:, col:col + n]

    def barrier(self):
        nc = self.nc
        d = self.dummy
        self.S.barrier(lambda: nc.vector.memset(d[:, 0:2], 0.0))

    def load_w(self, src, ncols, cast_eng="pool"):
        nc, S = self.nc, self.S
        i = self.wcount
        self.wcount += 1
        st, bf = i % 2, i % 3
        stt, bft = self.wst[st], self.wbf[bf]
        S.dma("sp", lambda: nc.sync.dma_start(out=stt[:, 0:ncols], in_=src), ("ws", st),
              writes=[("ws", st)])
        eng = {"pool": nc.gpsimd, "act": nc.scalar, "dve": nc.vector}[cast_eng]
        if cast_eng == "act":
            fn = lambda: nc.scalar.copy(out=bft[:, 0:ncols], in_=stt[:, 0:ncols])
        else:
            fn = lambda: eng.tensor_copy(out=bft[:, 0:ncols], in_=stt[:, 0:ncols])
        S.op(cast_eng, fn, reads=[("ws", st)], writes=[("wb", bf)])
        return bft, ("wb", bf)

    def mm(self, out, lhsT, rhs, start, stop, reads, writes, skip=False):
        nc = self.nc
        if skip:
            fn = lambda: nc.tensor.matmul(out, lhsT=lhsT, rhs=rhs, start=start, stop=stop,
                                          skip_group_check=True)
        else:
            fn = lambda: nc.tensor.matmul(out, lhsT=lhsT, rhs=rhs, start=start, stop=stop)
        self.S.op("pe", fn, reads=reads, writes=writes)

    def act(self, out, in_, func, reads, writes, scale=1.0, bias=None, accum=None):
        nc = self.nc
        kw = {}
        if bias is not None:
            kw["bias"] = bias
        if accum is not None:
            kw["accum_out"] = accum
        self.S.op("act", lambda: nc.scalar.activation(out=out, in_=in_, func=func, scale=scale, **kw),
                  reads=reads, writes=writes)

    def tt(self, eng, out, in0, in1, op, reads, writes):
        e = {"dve": self.nc.vector, "pool": self.nc.gpsimd}[eng]
        self.S.op(eng, lambda: e.tensor_tensor(out=out, in0=in0, in1=in1, op=op), reads=reads, writes=writes)

    def ts(self, eng, out, in0, s1, op0, reads, writes, s2=None, op1=None):
        e = {"dve": self.nc.vector, "pool": self.nc.gpsimd}[eng]
        if op1 is None:
            fn = lambda: e.tensor_scalar(out=out, in0=in0, scalar1=s1, scalar2=None, op0=op0)
        else:
            fn = lambda: e.tensor_scalar(out=out, in0=in0, scalar1=s1, scalar2=s2, op0=op0, op1=op1)
        self.S.op(eng, fn, reads=reads, writes=writes)

    def stt(self, out, in0, scalar, in1, op0, op1, reads, writes):
        nc = self.nc
        self.S.op("dve", lambda: nc.vector.scalar_tensor_tensor(out=out, in0=in0, scalar=scalar, in1=in1,
                                                                op0=op0, op1=op1), reads=reads, writes=writes)

    def copy(self, eng, out, in_, reads, writes):
        nc = self.nc
        if eng == "act":
            fn = lambda: nc.scalar.copy(out=out, in_=in_)
        else:
            e = {"dve": nc.vector, "pool": nc.gpsimd}[eng]
            fn = lambda: e.tensor_copy(out=out, in_=in_)
        self.S.op(eng, fn, reads=reads, writes=writes)

    def dma(self, out, in_, semkey, reads, writes, queue="sp"):
        nc = self.nc
        e = nc.sync if queue == "sp" else nc.scalar
        self.S.dma(queue, lambda: e.dma_start(out=out, in_=in_), semkey, reads=reads, writes=writes)

    def init(self):
        nc, S = self.nc, self.S
        self.dma(self.cst[:, :], self.cst_d, "cst", [], ["cst"])
        for k, col in enumerate((C_ONES, C_TRI)):
            self.copy("dve", self.cbf[:, k * 128:(k + 1) * 128], self.cst[:, col:col + 128], ["cst"], ["cbf"])
        stg = self.hT_raw[:, 0:1024].bitcast(F32)
        self.dma(stg, self.cmat_d, "cmat", [], ["cmat"])
        self.copy("dve", self.cbf[:, 256:768], stg, ["cmat"], ["cbf"])
        self.ones_bf = self.cbf[:, 0:128]
        self.tri_bf = self.cbf[:, 128:256]
        self.uinc_bf = self.cbf[:, 256:384]
        self.lstr_bf = self.cbf[:, 384:512]
        self.ident_bf = self.cbf[:, 512:640]
        self.negtri_bf = self.cbf[:, 640:768]

    def norm(self, src, gcol, tgs, hcol0, final=False):
        S = self.S
        srcv = src.rearrange("(c p) t -> p c t", p=128)
        slabs = [self.big[:, k * 16384:(k + 1) * 16384].bitcast(F32).rearrange("p (c t) -> p c t", c=NCH)
                 for k in range(2)]
        sq = [self.scr[:, k * 512:(k + 1) * 512] for k in range(2)]
        std = [self.scr[:, 1024 + k * 1024: 2048 + k * 1024].bitcast(F32) for k in range(2)]
        yv = self.y.rearrange("(c p) t -> p c t", p=128)
        for k, tg in enumerate(tgs):
            sl = slabs[k % 2]
            skq = [("slab", k % 2, q) for q in range(4)]
            for q in range(4):
                self.dma(sl[:, q * 4:(q + 1) * 4, :], srcv[:, q * 4:(q + 1) * 4, tg * 512:(tg + 1) * 512],
                         ("slab", k % 2, q), [("xres", c, tg) for c in range(q * 4, q * 4 + 4)], [skq[q]])
            b = k % 2
            for c in range(NCH):
                self.act(sq[c % 2], sl[:, c, :], AF.Square, [skq[c // 4]], [("sq", c % 2)])
                self.mm(self.bank(b), self.ones_bf, sq[c % 2], c == 0, c == NCH - 1,
                        [("sq", c % 2), "cbf"], [("ps", b)])
            st = std[k % 2]
            self.act(st, self.bank(b), AF.Sqrt, [("ps", b)], [("std", k % 2)], scale=1.0 / D, bias=self.eps_ap)
            self.S.op("dve", self._recip(st, st), reads=[("std", k % 2)], writes=[("std", k % 2)])
            for c in range(NCH):
                if final:
                    self.stt(sl[:, c, :], sl[:, c, :], self.cc(gcol + c), st, ALU.mult, ALU.mult,
                             [skq[c // 4], ("std", k % 2), "cst"], [skq[c // 4]])
                else:
                    lt = hcol0 // 512 + k
                    self.stt(self.hv[:, c, lt * 512:(lt + 1) * 512], sl[:, c, :], self.cc(gcol + c), st,
                             ALU.mult, ALU.mult, [skq[c // 4], ("std", k % 2), "cst"], [("hT", c, lt)])
            if final:
                for q in range(4):
                    self.dma(yv[:, q * 4:(q + 1) * 4, tg * 512:(tg + 1) * 512], sl[:, q * 4:(q + 1) * 4, :],
                             ("yout", k % 2, q), [skq[q]], [("y", q, tg)])

    def _recip(self, out, in_):
        nc = self.nc
        return lambda: nc.vector.reciprocal(out=out, in_=in_)

    def _sqsum(self, junk, x, acc):
        nc = self.nc
        return lambda: nc.vector.scalar_tensor_tensor(out=junk, in0=x, scalar=1.0, in1=x, op0=ALU.mult,
                                                      op1=ALU.mult, accum_out=acc)

    def _reduce(self, out, in_, axis):
        nc = self.nc
        return lambda: nc.vector.tensor_reduce(out=out, in_=in_, op=ALU.add, axis=axis)

    def xload(self, src, n, tg):
        j = self.xcount % 4
        self.xcount += 1
        self.dma(self.xadd[j], src[n * 128:(n + 1) * 128, tg * 512:(tg + 1) * 512], ("xl", j),
                 [("xres", n, tg)], [("xadd", j)], queue="act")
        return j

    def xadd_store(self, j, b, n, tg):
        xt = self.xadd[j]
        self.tt("dve", xt, self.bank(b), xt, ALU.add, [("ps", b), ("xadd", j)], [("xadd", j)])
        self.dma(self.xres[n * 128:(n + 1) * 128, tg * 512:(tg + 1) * 512], xt, ("xs", j),
                 [("xadd", j)], [("xres", n, tg)], queue="act")

    def out_proj(self, wd, src):
        tiles = [(n, tg) for n in range(NCH) for tg in range(4)]
        pend = {}
        for idx in range(2):
            pend[idx] = self.xload(src, *tiles[idx])
        w = None
        ws = WStream(self, [(wd[n], D, "dve") for n in range(NCH)])
        for idx, (n, tg) in enumerate(tiles):
            if tg == 0:
                w, wk = ws.get(n)
            if idx + 2 < len(tiles):
                pend[idx + 2] = self.xload(src, *tiles[idx + 2])
            b = idx % 4
            for c in range(NCH):
                self.mm(self.bank(b), w[:, c * 128:(c + 1) * 128], self.cat[:, c, tg * 512:(tg + 1) * 512],
                        c == 0, c == NCH - 1, [wk, ("cat", c, tg)], [("ps", b)])
            self.xadd_store(pend.pop(idx), b, n, tg)

    def proj_fm(self, w, wk, b, lt):
        for c in range(NCH):
            self.mm(self.bank(b), w[:, c * 128:(c + 1) * 128], self.hv[:, c, lt * 512:(lt + 1) * 512],
                    c == 0, c == NCH - 1, [wk, ("hT", c, lt)], [("ps", b)])

    def proj_tm(self, w, wk, b, slot, tt_, ncols=128, wstride=128):
        for c in range(NCH):
            self.mm(self.bank(b, ncols, slot * ncols), self.hT[:, c, tt_ * 128:(tt_ + 1) * 128],
                    w[:, c * wstride:c * wstride + ncols], c == 0, c == NCH - 1,
                    [wk, ("hT", c, tt_ // 4)], [("ps", b)])

    def qkv_bufs(self, st, vt1):
        if st == 0:
            qT = self.scr[:, 0:2048]
            kT = self.scr[:, 2048:4096]
            vt = self.scr[:, 4096:6144].rearrange("p (t d) -> p t d", t=16)
        else:
            xb = self.xadd_raw[:, :].bitcast(BF16)
            qT = xb[:, 0:2048]
            kT = xb[:, 2048:4096]
            vt = vt1.rearrange("p (t d) -> p t d", t=16)
        return qT, kT, vt

    def qkv_gen(self, wd, bq, bk, bv, bufs, st, pb=(0, 1)):
        qT, kT, vt = bufs
        for which, blk_ in (("q", bq), ("k", bk)):
            w, wk = self.load_w(wd[blk_], D)
            dst = qT if which == "q" else kT
            for tg in range(4):
                b = pb[tg % 2]
                for c in range(NCH):
                    self.mm(self.bank(b), w[:, c * 128:(c + 1) * 128], self.hv[:, c, tg * 512:(tg + 1) * 512],
                            c == 0, c == NCH - 1, [wk, ("hT", c, tg)], [("ps", b)])
                    if c % 4 == 3 and c != NCH - 1:
                        yield
                if which == "q":
                    self.act(dst[:, tg * 512:(tg + 1) * 512], self.bank(b), AF.Copy, [("ps", b)],
                             [("q", st, tg)], scale=SCALE)
                else:
                    self.copy("dve", dst[:, tg * 512:(tg + 1) * 512], self.bank(b), [("ps", b)], [("k", st, tg)])
                yield
        w, wk = self.load_w(wd[bv], D)
        for t4 in range(4):
            b = pb[t4 % 2]
            for s_ in range(4):
                self.proj_tm(w, wk, b, s_, t4 * 4 + s_)
                if s_ != 3:
                    yield
            self.copy("dve", vt[:, t4 * 4:(t4 + 1) * 4, :],
                      self.bank(b).rearrange("p (t d) -> p t d", t=4), [("ps", b)], [("v", st, t4)])
            yield

    def mixer0(self):
        S = self.S
        wd = self.win[0]
        big = self.big
        sp0 = 32768
        u_full = big[:, sp0:sp0 + 4104].bitcast(F32)
        gcsb = big[:, sp0 + 4104:sp0 + 5128].bitcast(F32)
        acc = big[:, sp0 + 5128:sp0 + 6152].bitcast(F32)
        nc = self.nc
        S.op("pool", lambda: nc.gpsimd.memset(u_full[:, 0:2], 0.0), writes=[("u", -1)])
        import os
        parts = os.environ.get("MIX0", "AB")
        for g in range(8 if "B" in parts else 0):
            wgb, kgb = self.load_w(wd[24 + g], D)
            wgc, kgc = self.load_w(wd[32 + g], D)
            whi, khi = self.load_w(wd[40 + g], D)
            for tg in range(4):
                b0 = (tg % 2) * 3
                self.proj_fm(wgb, kgb, b0, tg)
                self.proj_fm(wgc, kgc, b0 + 1, tg)
                self.proj_fm(whi, khi, b0 + 2, tg)
                self.copy("act", gcsb, self.bank(b0 + 1), [("ps", b0 + 1)], ["gcsb"])
                c0 = 2 + tg * 512
                self.tt("dve", u_full[:, c0:c0 + 512], self.bank(b0 + 2), gcsb, ALU.mult,
                        [("ps", b0 + 2), "gcsb"], [("u", tg)])
                wc = C_SCW + g * 3
                self.ts("dve", acc, u_full[:, c0:c0 + 512], self.cc(wc + 2), ALU.mult, [("u", tg), "cst"], ["acc"])
                self.stt(acc, u_full[:, c0 - 1:c0 + 511], self.cc(wc + 1), acc, ALU.mult, ALU.add,
                         [("u", tg), ("u", tg - 1), "acc", "cst"], ["acc"])
                self.stt(acc, u_full[:, c0 - 2:c0 + 510], self.cc(wc), acc, ALU.mult, ALU.add,
                         [("u", tg), ("u", tg - 1), "acc", "cst"], ["acc"])
                self.tt("dve", self.cat[:, 8 + g, tg * 512:(tg + 1) * 512], self.bank(b0), acc, ALU.mult,
                        [("ps", b0), "acc"], [("cat", 8 + g, tg), ("slab", 1, g // 2)])
        self.barrier()
        msk = big[:, sp0:sp0 + 2048]
        mst = big[:, sp0 + 2048:sp0 + 6144].bitcast(F32)
        self.dma(mst, self.cm_d[:, M_MSB:M_MSB + 2048], "cmm", [], ["mst"])
        self.copy("dve", msk, mst, ["mst"], ["msk"])
        self.barrier()
        o = sp0 + 2048
        zsb = [[big[:, o + (2 * sl + k) * 1024:o + (2 * sl + k + 1) * 1024].bitcast(F32) for k in range(2)]
               for sl in range(2)]
        o += 4096
        esb = [big[:, o + k * 1024:o + (k + 1) * 1024].bitcast(F32) for k in range(1)]
        o += 1024
        spb = [[big[:, o + (2 * sl + k) * 512:o + (2 * sl + k + 1) * 512] for k in range(2)] for sl in range(2)]
        o += 2048
        abf = [[big[:, o + sl * 512:o + (sl + 1) * 512]] * 2 for sl in range(2)]
        o += 1024
        vt1 = big[:, o:o + 2048]
        o += 2048
        assert o <= 45056
        bufsets = [self.qkv_bufs(0, None), self.qkv_bufs(1, vt1)]
        self.zcnt = 0
        self.ecnt = 0

        def sb_stream(h, G, sl, st, accb, ob):
            qT, kT, vt = bufsets[st]
            blocks = list(range(4 * G + 3, -1, -1))
            qs = qT[:, G * 512:(G + 1) * 512]
            nb = len(blocks)
            zbank = {}

            def front(i, cn):
                zb = 2 + self.zcnt % 2
                self.zcnt += 1
                e = esb[0]
                r = i - 4 * G
                self.mm(self.bank(zb), kT[:, i * 128:(i + 1) * 128], qs, True, r < 0,
                        [("k", st, i // 4), ("q", st, G)], [("ps", zb)])
                if r >= 0:
                    self.mm(self.bank(zb), self.ident_bf, msk[:, r * 512:(r + 1) * 512], False, True,
                            ["cbf", "msk"], [("ps", zb)])
                self.act(e, self.bank(zb), AF.Exp, [("ps", zb)], ["esb"])
                self.act(spb[sl][cn % 2], e, AF.Ln, ["esb"], [("sp", sl, cn % 2)], bias=self.one_ap)
                zbank[cn % 2] = zb

            def zc(cn):
                zb = zbank[cn % 2]
                self.copy("dve", zsb[sl][cn % 2], self.bank(zb), [("ps", zb)], [("zsb", sl, cn % 2)])

            def back_u(i, cn, first, last):
                self.mm(self.bank(accb), self.uinc_bf, spb[sl][cn % 2], first, False,
                        [("sp", sl, cn % 2), "cbf"], [("ps", accb)], skip=True)

            def back_b(i, cn, first, last):
                self.tt("dve", zsb[sl][cn % 2], zsb[sl][cn % 2], self.bank(accb), ALU.subtract,
                        [("zsb", sl, cn % 2), ("ps", accb)], [("zsb", sl, cn % 2)])
                if not last:
                    self.mm(self.bank(accb), self.lstr_bf, spb[sl][cn % 2], False, False,
                            [("sp", sl, cn % 2), "cbf"], [("ps", accb)], skip=True)
                self.act(abf[sl][cn % 2], zsb[sl][cn % 2], AF.Exp, [("zsb", sl, cn % 2)], [("abf", sl)])

            def back_av(i, cn, first, last):
                self.mm(self.bank(ob), vt[:, i, :], abf[sl][cn % 2], first, last,
                        [("v", st, i // 4), ("abf", sl)], [("ps", ob)])

            front(blocks[0], 0)
            yield
            zc(0)
            yield
            for k in range(nb):
                if k + 1 < nb:
                    front(blocks[k + 1], k + 1)
                    yield
                args = (blocks[k], k, k == 0, k == nb - 1)
                back_u(*args)
                yield
                back_b(*args)
                yield
                if k + 1 < nb:
                    zc(k + 1)
                back_av(*args)
                yield
            self.copy("dve", self.cat[:, h, G * 512:(G + 1) * 512], self.bank(ob), [("ps", ob)],
                      [("cat", h, G)])

        nh = int(os.environ.get("NH", "8")) if "A" in parts else 0
        if nh:
            for _ in self.qkv_gen(wd, 0, 8, 16, bufsets[0], 0):
                pass
        for h in range(nh):
            st = h % 2
            pg = self.qkv_gen(wd, h + 1, 9 + h, 17 + h, bufsets[1 - st], 1 - st) if h + 1 < nh else iter(())
            todo = [3, 2, 1, 0]
            free_slots = [(0, 4, 6), (1, 5, 7)]
            active = []
            steps = 0
            while todo or active:
                while todo and free_slots:
                    sl, accb, ob = free_slots.pop(0)
                    active.append((sb_stream(h, todo.pop(0), sl, st, accb, ob), (sl, accb, ob)))
                for item in list(active):
                    g, slot = item
                    try:
                        next(g)
                    except StopIteration:
                        active.remove(item)
                        free_slots.append(slot)
                    steps += 1
                    if steps % 3 == 0:
                        next(pg, None)
            for _ in pg:
                pass
        self.barrier()

    def ffn(self, layer, gcol):
        S = self.S
        wup, wdn = self.wup[layer], self.wdn[layer]
        fcw = C_FCW0 if layer == 0 else C_FCW1
        gT = self.big[:, 0:NFC * 1024].rearrange("p (f t) -> p f t", f=NFC)
        h2 = self.hT_raw[:, 16384:32768]
        cbuf = {}
        for k in range(2):
            cbuf[("g", k)] = h2[:, k * 4096:k * 4096 + 2048].bitcast(F32)
            cbuf[("u", k)] = h2[:, k * 4096 + 2048:k * 4096 + 4096].bitcast(F32)
        halo = h2[:, 8192:8192 + 352].bitcast(F32)
        self.hv = self.hTf
        for half in range(2):
            self.norm(self.xres, gcol, [half * 2, half * 2 + 1], 0)
            ws = WStream(self, [(wup[(jj // 2) + NFC * (jj % 2)], D, "act") for jj in range(2 * NFC)])
            for j in range(NFC):
                k = j % 2
                for which, blk_, b0 in (("g", j, 4 * k), ("u", NFC + j, 4 * k + 2)):
                    w, wk = ws.get(2 * j + (0 if which == "g" else 1))
                    for lt in range(2):
                        self.proj_fm(w, wk, b0 + lt, lt)
                    P = self.ps[:, b0 * 512:b0 * 512 + 1024]
                    pk = [("ps", b0), ("ps", b0 + 1)]
                    cb = cbuf[(which, k)]
                    ck = ("cb", which, k)
                    wc = fcw + blk_ * 3
                    self.act(cb, P, AF.Identity, pk, [ck], scale=self.cc(wc + 2))
                    self.stt(cb[:, 1:1024], P[:, 0:1023], self.cc(wc + 1), cb[:, 1:1024], ALU.mult, ALU.add,
                             pk + [ck, "cst"], [ck])
                    self.stt(cb[:, 2:1024], P[:, 0:1022], self.cc(wc), cb[:, 2:1024], ALU.mult, ALU.add,
                             pk + [ck, "cst"], [ck])
                    hl = halo[:, blk_ * 2:blk_ * 2 + 2]
                    if half == 0:
                        self.copy("act", hl, P[:, 1022:1024], pk, [("halo", blk_)])
                    else:
                        hk = ("halo", blk_)
                        self.stt(cb[:, 0:1], hl[:, 1:2], self.cc(wc + 1), cb[:, 0:1], ALU.mult, ALU.add,
                                 [hk, ck, "cst"], [ck])
                        self.stt(cb[:, 0:1], hl[:, 0:1], self.cc(wc), cb[:, 0:1], ALU.mult, ALU.add,
                                 [hk, ck, "cst"], [ck])
                        self.stt(cb[:, 1:2], hl[:, 1:2], self.cc(wc), cb[:, 1:2], ALU.mult, ALU.add,
                                 [hk, ck, "cst"], [ck])
                cg, cu = cbuf[("g", k)], cbuf[("u", k)]
                self.act(cg, cg, AF.Silu, [("cb", "g", k)], [("cb", "g", k)])
                self.tt("pool", gT[:, j, :], cg, cu, ALU.mult, [("cb", "g", k), ("cb", "u", k)],
                        [("gT", j)] + ([("slab", j // 16, (j % 16) // 4)] if j < 32 else []))
            tiles = [(n, lt) for n in range(NCH) for lt in range(2)]
            pend = {}
            for idx in range(2):
                n, lt = tiles[idx]
                pend[idx] = self.xload(self.xres, n, half * 2 + lt)
            ws = WStream(self, [(wdn[nn // 4][:, (nn % 4) * 1408:(nn % 4 + 1) * 1408], 1408, "dve")
                                for nn in range(4 * NCH)])
            for n in range(NCH):
                b0 = (n % 4) * 2
                for piece in range(4):
                    w, wk = ws.get(n * 4 + piece)
                    for lt in range(2):
                        for fl in range(11):
                            fc = piece * 11 + fl
                            self.mm(self.bank(b0 + lt), w[:, fl * 128:(fl + 1) * 128],
                                    gT[:, fc, lt * 512:(lt + 1) * 512], fc == 0, fc == NFC - 1,
                                    [wk, ("gT", fc)], [("ps", b0 + lt)])
                for lt in range(2):
                    idx = n * 2 + lt
                    if idx + 2 < len(tiles):
                        n2, lt2 = tiles[idx + 2]
                        pend[idx + 2] = self.xload(self.xres, n2, half * 2 + lt2)
                    self.xadd_store(pend.pop(idx), b0 + lt, n, half * 2 + lt)
            self.barrier()
        self.hv = self.hT

    def mixer1(self):
        S, nc = self.S, self.nc
        wd = self.win[1]
        big = self.big
        sp0 = 32768
        bsg = big[:, sp0:sp0 + 2048].bitcast(F32)
        sgg = big[:, sp0 + 2048:sp0 + 3072]
        wst = big[:, sp0 + 3072:sp0 + 4096]
        o = sp0 + 4096
        stg = big[:, o:o + 2048].bitcast(F32)
        o += 2048
        ug = big[:, o:o + 2048]
        o += 2048
        tmp = [big[:, o + k * 1024:o + (k + 1) * 1024].bitcast(F32) for k in range(2)]
        o += 2048
        small = big[:, o:o + 2048].bitcast(F32)
        gv = big[:, 8 * T:16 * T].rearrange("p (t c) -> p t c", t=16)
        GV = "gv"
        self.dma(bsg, self.cm_d[:, M_BSG:M_BSG + 1024], "cm0", [], ["bsg"])
        self.dma(stg, self.cm_d[:, M_SGG:M_SGG + 1024], "cm1", [], ["stg"])
        self.copy("dve", sgg, stg, ["stg"], ["sgg"])
        self.dma(stg, self.cm_d[:, M_WST:M_WST + 1024], "cm2", [], ["stg"])
        for g in range(8):
            self.tt("dve", wst[:, g * 128:(g + 1) * 128], stg[:, g * 128:(g + 1) * 128],
                    self.cc(C_TRI, 128), ALU.mult, ["stg", "cst"], ["wst"])
        AX = mybir.AxisListType.X
        s1p = small[:, 128:256]
        s2p = small[:, 256:384]
        junk = small[:, 384:448].bitcast(BF16)
        s1 = small[:, 0:16]
        s2 = small[:, 16:32]
        mean = small[:, 32:48]
        msq = small[:, 48:64]
        var = small[:, 64:80]
        for bb in range(8):
            w, wk = self.load_w(wd[8 + bb], D)
            for t4 in range(4):
                b = 2 + t4 % 2
                for s in range(4):
                    self.proj_tm(w, wk, b, s, t4 * 4 + s)
                for s in range(4):
                    tt_ = t4 * 4 + s
                    gsl = gv[:, tt_, bb * 128:(bb + 1) * 128]
                    col = tt_ * 8 + bb
                    self.act(gsl, self.bank(b, 128, s * 128), AF.Gelu, [("ps", b)],
                             [("gvp", tt_ % 4), ("slab", 1, tt_ // 4)], accum=s1p[:, col:col + 1])
                    self.S.op("dve", self._sqsum(junk, gsl, s2p[:, col:col + 1]), reads=[("gvp", tt_ % 4)],
                              writes=["junk", "s2p"])
        self.S.op("dve", self._reduce(s1, s1p.rearrange("p (t b) -> p t b", b=8), AX),
                  reads=[("gvp", k) for k in range(4)] + ["s2p"], writes=["s1"])
        self.S.op("dve", self._reduce(s2, s2p.rearrange("p (t b) -> p t b", b=8), AX), reads=["s2p"], writes=["s2"])
        self.ts("dve", mean, s1, 1.0 / 1024, ALU.mult, ["s1"], ["mean"])
        self.tt("dve", msq, mean, mean, ALU.mult, ["mean"], ["msq"])
        self.stt(var, s2, 1.0 / 1024, msq, ALU.mult, ALU.subtract, ["s2", "msq"], ["var"])
        self.act(var, var, AF.Sqrt, ["var"], ["var"], bias=self.eps_ap)
        self.S.op("dve", self._recip(var, var), reads=["var"], writes=["var"])
        ugs = [ug, stg.bitcast(BF16)[:, 0:2048]]

        def uproj(g):
            w, wk = self.load_w(wd[g], D)
            for tg in range(4):
                b = tg % 2
                self.proj_fm(w, wk, b, tg)
                self.act(ugs[g % 2][:, tg * 512:(tg + 1) * 512], self.bank(b), AF.Gelu, [("ps", b)],
                         [("ug", g % 2, tg)] + (["stg"] if g == 1 else []))

        def mixing(g):
            for tg in range(4):
                b = 4 + tg % 2
                for s in range(4):
                    n = tg * 4 + s
                    self.mm(self.bank(b, 128, s * 128), gv[:, n, g * 128:(g + 1) * 128],
                            wst[:, g * 128:(g + 1) * 128], True, True, [GV, "wst"], [("ps", b)])
                tm = tmp[tg % 2]
                self.tt("dve", tm.rearrange("p (s t) -> p s t", s=4),
                        self.bank(b).rearrange("p (s t) -> p s t", s=4),
                        bsg[:, g * 128:(g + 1) * 128].unsqueeze(1).to_broadcast([128, 4, 128]), ALU.add,
                        [("ps", b), "bsg"], [("tmp", tg % 2)])
                self.tt("dve", self.cat[:, g, tg * 512:(tg + 1) * 512], tm, ugs[g % 2][:, tg * 512:(tg + 1) * 512],
                        ALU.mult, [("tmp", tg % 2), ("ug", g % 2, tg)], [("cat", g, tg), ("slab", 0, g // 2)])

        uproj(0)
        uproj(1)
        for tt_ in range(16):
            self.ts("dve", gv[:, tt_, :], gv[:, tt_, :], mean[:, tt_:tt_ + 1], ALU.subtract,
                    [("gvp", k) for k in range(4)] + ["mean", "var"], [GV],
                    s2=var[:, tt_:tt_ + 1], op1=ALU.mult)
            self.tt("dve", gv[:, tt_, :], gv[:, tt_, :], sgg, ALU.mult, [GV, "sgg"], [GV])
        for g in range(8):
            mixing(g)
            if g + 2 < 8:
                uproj(g + 2)
        self.barrier()
        xf = small[:, 128:256]
        lf = small[:, 256:384]
        pfx = small[:, 384:512]
        negc = small[:, 512:640]
        rs = small[:, 640:768]
        bt = small[:, 768:1024]
        wf, wfk = self.load_w(self.wf_d, 128, cast_eng="dve")
        for tt_ in range(16):
            self.proj_tm(wf, wfk, 0, tt_, tt_, ncols=8, wstride=8)
        self.tt("dve", xf, self.bank(0, 128), self.cc(C_FOXB, 128), ALU.add, [("ps", 0), "cst"], ["xf"])
        self.act(xf, xf, AF.Exp, ["xf"], ["xf"], scale=-1.0)
        self.act(lf, xf, AF.Ln, ["xf"], ["lf"], bias=self.one_ap)
        self.mm(self.bank(1, 128), self.cc(C_TRI, 128), lf, True, True, ["lf", "cst"], [("ps", 1)])
        self.mm(self.bank(2, 128), self.cc(C_ONES, 128), lf, True, True, ["lf", "cst"], [("ps", 2)])
        S.op("dve", lambda: nc.vector.memset(pfx[:, 0:8], 0.0), writes=["pfx"])
        for tt_ in range(1, 16):
            self.tt("dve", pfx[:, tt_ * 8:(tt_ + 1) * 8], pfx[:, (tt_ - 1) * 8:tt_ * 8],
                    self.bank(2, 8, (tt_ - 1) * 8), ALU.add, ["pfx", ("ps", 2)], ["pfx"])
        self.tt("dve", negc, pfx, self.bank(1, 128), ALU.add, ["pfx", ("ps", 1)], ["negc"])
        self.mm(self.bank(3, 128), self.cc(C_SEL, 128), negc, True, True, ["negc", "cst"], [("ps", 3)])
        self.copy("dve", rs, self.bank(3, 128), [("ps", 3)], ["rs"])
        self.barrier()
        pbf = [stg.bitcast(BF16)[:, k * 128:(k + 1) * 128] for k in range(8)]
        rden = tmp[0][:, 0:128]
        rsv = rs.rearrange("p (t h) -> p t h", h=8)
        bts = [bt, small[:, 128:384]]
        vt1 = big[:, sp0:sp0 + 2048]
        bufsets = [self.qkv_bufs(0, None), self.qkv_bufs(1, vt1)]
        blocks = [(tb, i) for tb in range(16) for i in range(tb + 1)]
        LA = 4

        def fox_gen(h, st):
            qT, kT, vt = bufsets[st]
            btt = bts[h % 2]
            btk = ("bt", h % 2)
            for i in range(16):
                self.ts("dve", btt[:, i * 16:(i + 1) * 16], rsv[:, :, h], -1.0, ALU.mult, ["rs", "negc"], [btk],
                        s2=negc[:, i * 8 + h:i * 8 + h + 1], op1=ALU.add)
            yield

            def zstep(n):
                tb, i = blocks[n]
                zb = 1 + n % 5
                p, pk = pbf[n % 8], ("pbf", n % 8)
                self.mm(self.bank(zb, 128), kT[:, i * 128:(i + 1) * 128], qT[:, tb * 128:(tb + 1) * 128],
                        True, i != tb, [("k", st, i // 4), ("q", st, tb // 4)], [("ps", zb)])
                if i == tb:
                    self.mm(self.bank(zb, 128), self.ident_bf, self.negtri_bf, False, True, ["cbf"], [("ps", zb)])
                self.act(p, self.bank(zb, 128), AF.Exp, [("ps", zb), btk], [pk],
                         bias=btt[:, i * 16 + tb:i * 16 + tb + 1])

            def ostep(n):
                tb, i = blocks[n]
                ob = 6 + tb % 2
                p, pk = pbf[n % 8], ("pbf", n % 8)
                self.mm(self.bank(ob, 128), vt[:, i, :], p, i == 0, i == tb, [("v", st, i // 4), pk],
                        [("ps", ob)], skip=True)
                self.mm(self.bank(ob, 128, 128), self.ones_bf, p, False, i == tb, [pk, "cbf"], [("ps", ob)],
                        skip=True)
                if i == tb:
                    self.S.op("dve", self._recip(rden, self.bank(ob, 128, 128)), reads=[("ps", ob)],
                              writes=["rden"])
                    self.tt("dve", self.cat[:, 8 + h, tb * 128:(tb + 1) * 128], self.bank(ob, 128), rden,
                            ALU.mult, [("ps", ob), "rden"], [("cat", 8 + h, tb // 4), GV])

            for n in range(len(blocks) + LA):
                if n < len(blocks):
                    zstep(n)
                if n - LA >= 0:
                    ostep(n - LA)
                yield

        for _ in self.qkv_gen(wd, 16, 24, 32, bufsets[0], 0, pb=(0, 0)):
            pass
        for h in range(8):
            st = h % 2
            pg = self.qkv_gen(wd, 17 + h, 25 + h, 33 + h, bufsets[1 - st], 1 - st, pb=(0, 0)) if h + 1 < 8 else iter(())
            steps = 0
            for _ in fox_gen(h, st):
                steps += 1
                if steps % 3 == 0:
                    next(pg, None)
            for _ in pg:
                pass
        self.barrier()

    def build(self, upto=99):
        nc, S = self.nc, self.S
        self.init()
        S.op("dve", lambda: nc.vector.memset(self.dummy[:, 2:3], EPS), writes=["dummy"])
        S.op("dve", lambda: nc.vector.memset(self.dummy[:, 3:4], 1.0), writes=["dummy"])
        self.eps_ap = self.dummy[:, 2:3]
        self.one_ap = self.dummy[:, 3:4]
        self.barrier()
        steps = [
            lambda: self.norm(self.xT, C_G + 0, [0, 1, 2, 3], 0),
            lambda: self.mixer0(),
            lambda: self.out_proj(self.wout[0], self.xT),
            lambda: self.ffn(0, C_G + 16),
            lambda: self.norm(self.xres, C_G + 32, [0, 1, 2, 3], 0),
            lambda: self.mixer1(),
            lambda: self.out_proj(self.wout[1], self.xres),
            lambda: self.ffn(1, C_G + 48),
            lambda: self.norm(self.xres, C_G + 64, [0, 1, 2, 3], 0, final=True),
        ]
        for i, st in enumerate(steps):
            if i > upto:
                break
            st()
            if i not in (0, 4):
                self.barrier()
        if self.debug:
            self.dma(self.dbg_h, self.hT_raw[:, :], "dbgh", [], [])
            self.dma(self.dbg_c, self.big[:, :], "dbgc", [], [])
        info = S.emit()
        return nc, info


_CACHE = {}


def _prep_shared(inp):
    f = lambda k: np.asarray(inp[k], dtype=np.float32)
    cst = np.zeros((128, NCST), np.float32)
    for i, k in enumerate(("l0_mix_norm_g", "l0_ffn_norm_g", "l1_mix_norm_g", "l1_ffn_norm_g", "final_norm_g")):
        cst[:, C_G + 16 * i:C_G + 16 * (i + 1)] = _pc(f(k))
    scw = f("l0_sc_conv_w")
    cst[:, C_SCW:C_SCW + 24] = scw.reshape(3, 8, 128).transpose(2, 1, 0).reshape(128, 24)
    for col, k in ((C_FCW0, "l0_ffn_conv_w"), (C_FCW1, "l1_ffn_conv_w")):
        cw = f(k)
        cst[:, col:col + 264] = cw.reshape(3, 88, 128).transpose(2, 1, 0).reshape(128, 264)
    cst[:, C_FOXB:C_FOXB + 128] = np.tile(f("l1_fox_b_f"), 16)[None, :]
    r = np.arange(128)
    cst[:, C_ONES:C_ONES + 128] = 1.0
    cst[:, C_TRI:C_TRI + 128] = (r[:, None] <= r[None, :])
    cst[:, C_SEL:C_SEL + 128] = (r[:, None] == 64)
    cmat = np.zeros((128, 512), np.float32)
    cmat[:, 0:128] = (r[:, None] >= r[None, :])
    cmat[:, 128:256] = (r[:, None] < r[None, :])
    cmat[:, 256:384] = (r[:, None] == r[None, :])
    cmat[:, 384:512] = np.where(r[:, None] <= r[None, :], 0.0, -30000.0)
    cm = np.zeros((128, NCM), np.float32)
    tcol = np.arange(512)
    for q in range(4):
        cm[:, M_MSB + q * 512:M_MSB + (q + 1) * 512] = np.where((128 * q + r[:, None]) < tcol[None, :], 0.0, -30000.0)
    cm[:, M_BSG:M_BSG + 1024] = f("l1_sg_b").reshape(1, 1024)
    cm[:, M_SGG:M_SGG + 1024] = f("l1_sg_norm_g").reshape(1, 1024)
    cm[:, M_WST:M_WST + 1024] = f("l1_sg_w").transpose(2, 0, 1).reshape(128, 1024)
    w1 = f("l1_w_in")
    sh = {
        "cst": cst, "cm": cm, "cmat": cmat,
        "l0_win": _blk(f("l0_w_in")), "l1_win": _blk(w1[:, :5120]),
        "l1_wf": np.ascontiguousarray(w1[:, 5120:5128].reshape(16, 128, 8).transpose(1, 0, 2).reshape(128, 128)),
        "l0_wout": _blk(f("l0_w_out")), "l1_wout": _blk(f("l1_w_out")),
        "l0_up": _blk(f("l0_ffn_up")), "l1_up": _blk(f("l1_ffn_up")),
        "l0_dn": _blk(f("l0_ffn_down")), "l1_dn": _blk(f("l1_ffn_down")),
    }
    return sh


def kernel(**inputs):
    x = np.asarray(inputs["x"], dtype=np.float32)
    if "nc" not in _CACHE:
        _CACHE["nc"], _CACHE["info"] = Builder().build()
    nc = _CACHE["nc"]
    sh = _prep_shared(inputs)
    in_maps = []
    for b in range(8):
        m = dict(sh)
        m["xT"] = np.ascontiguousarray(x[b].T)
        in_maps.append(m)
    res = run_bass_kernel_spmd(nc, in_maps, core_ids=list(range(8)))
    out = np.stack([np.ascontiguousarray(res.results[b]["y"].T) for b in range(8)], axis=0)
    return out.astype(np.float32)
```
